# Optimizing a Trainium2 kernel written in Bass

```python
import math
import jax, jax.numpy as jnp
from jax import lax
import numpy as np

D_MODEL = 1024
BATCH = 4
SEQ = 8192
DEPTH = 1
DEC_BATCH = 16
DEC_SEQ = 16
PAST_LEN = 1024

CHUNK = 64
N_META = 16
PREFIX = 128
Q_BLOCK = 128
RET_HEADS = 4
RET_DK = D_MODEL // 8
RET_DV = 2 * RET_DK
RET_QK_W = RET_HEADS * RET_DK
RET_V_W = RET_HEADS * RET_DV
ROPE_BASE = 10000.0
RET_EPS = 1e-6
DIFF_HEADS = 8
DIFF_DH = D_MODEL // 16
DIFF_DV = 2 * DIFF_DH
DIFF_W = DIFF_HEADS * DIFF_DV
DIFF_EPS = 1e-5
N_BUCKETS = 32
MAX_DISTANCE = 128
D_FF = ((8 * D_MODEL + 3 * 256 - 1) // (3 * 256)) * 256
NORM_EPS = 1e-6
NEG_INF = -1e30
IN_WIDTHS = (RET_QK_W, RET_QK_W, RET_V_W, RET_V_W,
             DIFF_HEADS * 2 * DIFF_DH, DIFF_HEADS * 2 * DIFF_DH, DIFF_W,
             D_MODEL, D_MODEL)
W_IN = sum(IN_WIDTHS)

kernel_name = 'hybrid_retention_diffattn_stream'


def rmsnorm(x, g=None, eps=NORM_EPS):
    xf = x.astype(jnp.float32)
    y = xf * lax.rsqrt(jnp.mean(xf * xf, axis=-1, keepdims=True) + eps)
    if g is not None:
        y = y * g.astype(jnp.float32)
    return y.astype(x.dtype)


def in_proj(xn, w_in):
    offsets = np.cumsum(IN_WIDTHS)[:-1].tolist()
    return jnp.split(xn @ w_in, offsets, axis=-1)


def rotary(x, pos):
    half = x.shape[-1] // 2
    freq = 1.0 / (ROPE_BASE ** jnp.linspace(0.0, 1.0, half, dtype=jnp.float32))
    ang = pos.astype(jnp.float32)[:, None] * freq[None, :]
    cos = jnp.cos(ang)[None, :, None, :]
    sin = jnp.sin(ang)[None, :, None, :]
    x1, x2 = x[..., :half], x[..., half:]
    return jnp.concatenate([x1 * cos - x2 * sin, x1 * sin + x2 * cos], axis=-1).astype(x.dtype)


def retention_log_decay():
    return jnp.log(1.0 - 2.0 ** (-5.0 - jnp.arange(RET_HEADS, dtype=jnp.float32)))


def retention_chunk(S, q, k, v):
    L = q.shape[1]
    lg = retention_log_decay()
    i = jnp.arange(L, dtype=jnp.float32)
    qf, kf, vf, Sf = (t.astype(jnp.float32) for t in (q, k, v, S))
    decay_in = jnp.exp(lg[:, None, None] * jnp.abs(i[:, None] - i[None, :]))
    s = jnp.einsum('blhd,bmhd->bhlm', qf, kf) * decay_in[None]
    o = jnp.einsum('bhlm,bmhe->blhe', s, vf)
    q_dec = jnp.exp(lg[None, :] * (i[:, None] + 1.0))
    o = o + jnp.einsum('blhd,bhde->blhe', qf * q_dec[None, :, :, None], Sf)
    k_dec = jnp.exp(lg[None, :] * (L - 1.0 - i)[:, None])
    S_new = jnp.exp(lg * L)[None, :, None, None] * Sf + jnp.einsum('bmhd,bmhe->bhde', kf * k_dec[None, :, :, None], vf)
    return o.astype(q.dtype), S_new.astype(S.dtype)


def t5_bucket(rel):
    nb = N_BUCKETS // 2
    max_exact = nb // 2
    ret = jnp.where(rel > 0, nb, 0)
    n = jnp.abs(rel)
    nf = jnp.maximum(n, max_exact).astype(jnp.float32)
    large = max_exact + (jnp.log(nf / max_exact) / math.log(MAX_DISTANCE / max_exact) * (nb - max_exact)).astype(jnp.int32)
    large = jnp.minimum(large, nb - 1)
    return ret + jnp.where(n < max_exact, n, large)


def diff_lambda(lq1, lk1, lq2, lk2, lam_init):
    f = lambda a, b: jnp.exp(jnp.sum(a.astype(jnp.float32) * b.astype(jnp.float32)))
    return f(lq1, lk1) - f(lq2, lk2) + lam_init


def diff_softmax_mix(q, k, v, bias, mask, lam):
    s = jnp.einsum('bqhcd,bkhcd->bchqk', q, k).astype(jnp.float32) * (DIFF_DH ** -0.5) + bias.astype(jnp.float32)
    s = jnp.where(mask, s, NEG_INF)
    p = jax.nn.softmax(s, axis=-1)
    a = p[:, 0] - lam * p[:, 1]
    return jnp.einsum('bhqk,bkhe->bqhe', a.astype(v.dtype), v)


def layer_tail(h, o_ret, rg, o_diff, gr, gd, lam_init, lw):
    (_, _, _, _, _, _, subg, w_rb, w_db, w_o, n2, w_up, w_down) = lw
    B, L, _ = h.shape
    y_ret = (rmsnorm(o_ret, None, RET_EPS) * jax.nn.silu(rg.reshape(B, L, RET_HEADS, RET_DV))).reshape(B, L, RET_V_W)
    y_diff = (rmsnorm(o_diff, subg, DIFF_EPS) * (1.0 - lam_init)).reshape(B, L, DIFF_W)
    merged = jax.nn.sigmoid(gr) * (y_ret @ w_rb) + jax.nn.sigmoid(gd) * (y_diff @ w_db)
    h = h + merged @ w_o
    gate, up = jnp.split(rmsnorm(h, n2) @ w_up, 2, axis=-1)
    return h + (jax.nn.silu(gate) * up) @ w_down


def prompt_layer(h, lw, lam_init, rel_bias):
    (n1, w_in, lq1, lk1, lq2, lk2) = lw[:6]
    B, LT, _ = h.shape
    xn = rmsnorm(h, n1)
    rq, rk, rv, rg, dq, dk, dv, gr, gd = in_proj(xn, w_in)
    idx = jnp.arange(LT)
    pos = idx - PREFIX
    valid = idx >= PREFIX - N_META
    q = rotary(rq.reshape(B, LT, RET_HEADS, RET_DK), pos) * (RET_DK ** -0.5)
    k = rotary(rk.reshape(B, LT, RET_HEADS, RET_DK), pos) * valid[None, :, None, None].astype(h.dtype)
    v = rv.reshape(B, LT, RET_HEADS, RET_DV)
    nc = LT // CHUNK
    chunks = lambda t: jnp.moveaxis(t.reshape((B, nc, CHUNK) + t.shape[2:]), 1, 0)

    def ret_step(S, qkv):
        o, S = retention_chunk(S, *qkv)
        return S, o

    S0 = jnp.zeros((B, RET_HEADS, RET_DK, RET_DV), h.dtype)
    S_fin, o_ret = lax.scan(ret_step, S0, (chunks(q), chunks(k), chunks(v)))
    o_ret = jnp.moveaxis(o_ret, 0, 1).reshape(B, LT, RET_HEADS, RET_DV)
    lam = diff_lambda(lq1, lk1, lq2, lk2, lam_init)
    qd = dq.reshape(B, LT, DIFF_HEADS, 2, DIFF_DH)
    kd = dk.reshape(B, LT, DIFF_HEADS, 2, DIFF_DH)
    vd = dv.reshape(B, LT, DIFF_HEADS, DIFF_DV)
    kchunk = jnp.floor_divide(pos, CHUNK)

    def attn_block(bi):
        start = bi * Q_BLOCK
        qb = lax.dynamic_slice_in_dim(qd, start, Q_BLOCK, axis=1)
        qpos = start + jnp.arange(Q_BLOCK) - PREFIX
        mask = (kchunk[None, :] <= jnp.floor_divide(qpos, CHUNK)[:, None]) & valid[None, :]
        bias = rel_bias.T[:, t5_bucket(pos[None, :] - qpos[:, None])]
        return diff_softmax_mix(qb, kd, vd, bias, mask, lam)

    o_diff = lax.map(attn_block, jnp.arange(LT // Q_BLOCK))
    o_diff = jnp.moveaxis(o_diff, 0, 1).reshape(B, LT, DIFF_HEADS, DIFF_DV)
    h = layer_tail(h, o_ret, rg, o_diff, gr, gd, lam_init, lw)
    keep = PREFIX - N_META
    k_rows = dk[:, keep:].reshape(B, LT - keep, DIFF_HEADS, 2 * DIFF_DH)
    v_rows = vd[:, keep:]
    return h, k_rows, v_rows, S_fin


def sample_layer(h, k_cache, v_cache, S, lw, lam_init, rel_bias):
    (n1, w_in, lq1, lk1, lq2, lk2) = lw[:6]
    B, L, _ = h.shape
    P = k_cache.shape[1]
    xn = rmsnorm(h, n1)
    rq, rk, rv, rg, dq, dk, dv, gr, gd = in_proj(xn, w_in)
    pos = P + jnp.arange(L)
    q = rotary(rq.reshape(B, L, RET_HEADS, RET_DK), pos) * (RET_DK ** -0.5)
    k = rotary(rk.reshape(B, L, RET_HEADS, RET_DK), pos)
    v = rv.reshape(B, L, RET_HEADS, RET_DV)
    o_ret, S_new = retention_chunk(S, q, k, v)
    lam = diff_lambda(lq1, lk1, lq2, lk2, lam_init)
    qd = dq.reshape(B, L, DIFF_HEADS, 2, DIFF_DH)
    kd = dk.reshape(B, L, DIFF_HEADS, 2, DIFF_DH)
    vd = dv.reshape(B, L, DIFF_HEADS, DIFF_DV)
    k_all = jnp.concatenate([k_cache.reshape(B, P, DIFF_HEADS, 2, DIFF_DH), kd], axis=1)
    v_all = jnp.concatenate([v_cache, vd], axis=1)
    kpos = jnp.arange(P + L)
    mask = jnp.floor_divide(kpos, CHUNK)[None, :] <= jnp.floor_divide(pos, CHUNK)[:, None]
    bias = rel_bias.T[:, t5_bucket(kpos[None, :] - pos[:, None])]
    o_diff = diff_softmax_mix(qd, k_all, v_all, bias, mask, lam)
    h = layer_tail(h, o_ret, rg, o_diff, gr, gd, lam_init, lw)
    return h, dk.reshape(B, L, DIFF_HEADS, 2 * DIFF_DH), vd, S_new


def setup_inputs(seed: int = 0) -> dict:
    key = jax.random.key(seed)
    ks = jax.random.split(key, 21)
    nrm = lambda k, shape, s: jax.random.normal(k, shape, jnp.float32) * s
    return {
        'x_prompt': nrm(ks[0], (BATCH, SEQ, D_MODEL), 1.0),
        'x_sample': nrm(ks[1], (DEC_BATCH, DEC_SEQ, D_MODEL), 1.0),
        'cache_k': nrm(ks[2], (DEPTH, DEC_BATCH, PAST_LEN, DIFF_HEADS, 2 * DIFF_DH), 1.0),
        'cache_v': nrm(ks[3], (DEPTH, DEC_BATCH, PAST_LEN, DIFF_HEADS, DIFF_DV), 1.0),
        'state_ret': nrm(ks[4], (DEPTH, DEC_BATCH, RET_HEADS, RET_DK, RET_DV), 4.0),
        'meta_tokens': nrm(ks[5], (N_META, D_MODEL), 1.0),
        'rel_bias': nrm(ks[6], (N_BUCKETS, DIFF_HEADS), 0.5),
        'norm1_g': 1.0 + nrm(ks[7], (DEPTH, D_MODEL), 0.02),
        'w_in': nrm(ks[8], (DEPTH, D_MODEL, W_IN), D_MODEL ** -0.5),
        'lambda_q1': nrm(ks[9], (DEPTH, DIFF_DH), 0.1),
        'lambda_k1': nrm(ks[10], (DEPTH, DIFF_DH), 0.1),
        'lambda_q2': nrm(ks[11], (DEPTH, DIFF_DH), 0.1),
        'lambda_k2': nrm(ks[12], (DEPTH, DIFF_DH), 0.1),
        'diff_subln_g': 1.0 + nrm(ks[13], (DEPTH, DIFF_DV), 0.02),
        'w_ret_branch': nrm(ks[14], (DEPTH, RET_V_W, D_MODEL), RET_V_W ** -0.5),
        'w_diff_branch': nrm(ks[15], (DEPTH, DIFF_W, D_MODEL), DIFF_W ** -0.5),
        'w_o': nrm(ks[16], (DEPTH, D_MODEL, D_MODEL), D_MODEL ** -0.5),
        'norm2_g': 1.0 + nrm(ks[17], (DEPTH, D_MODEL), 0.02),
        'w_ffn_up': nrm(ks[18], (DEPTH, D_MODEL, 2 * D_FF), D_MODEL ** -0.5),
        'w_ffn_down': nrm(ks[19], (DEPTH, D_FF, D_MODEL), D_FF ** -0.5),
        'normf_g': 1.0 + nrm(ks[20], (D_MODEL,), 0.02),
    }


def reference(x_prompt, x_sample, cache_k, cache_v, state_ret, meta_tokens, rel_bias, norm1_g, w_in,
              lambda_q1, lambda_k1, lambda_q2, lambda_k2, diff_subln_g, w_ret_branch, w_diff_branch,
              w_o, norm2_g, w_ffn_up, w_ffn_down, normf_g):
    B = x_prompt.shape[0]
    pad = jnp.zeros((B, PREFIX - N_META, D_MODEL), x_prompt.dtype)
    meta = jnp.broadcast_to(meta_tokens.astype(x_prompt.dtype)[None], (B, N_META, D_MODEL))
    h_p = jnp.concatenate([pad, meta, x_prompt], axis=1)
    h_s = x_sample
    kp, vp, sp, ksm, vsm, ssm = [], [], [], [], [], []
    for l in range(DEPTH):
        lw = (norm1_g[l], w_in[l], lambda_q1[l], lambda_k1[l], lambda_q2[l], lambda_k2[l], diff_subln_g[l],
              w_ret_branch[l], w_diff_branch[l], w_o[l], norm2_g[l], w_ffn_up[l], w_ffn_down[l])
        lam_init = 0.8 - 0.6 * math.exp(-0.3 * l)
        h_p, k_new, v_new, s_new = prompt_layer(h_p, lw, lam_init, rel_bias)
        kp.append(k_new)
        vp.append(v_new)
        sp.append(s_new)
        h_s, k_new, v_new, s_new = sample_layer(h_s, cache_k[l], cache_v[l], state_ret[l], lw, lam_init, rel_bias)
        ksm.append(k_new)
        vsm.append(v_new)
        ssm.append(s_new)
    y_prompt = rmsnorm(h_p[:, PREFIX:], normf_g)
    y_sample = rmsnorm(h_s, normf_g)
    return (y_prompt, y_sample, jnp.stack(kp), jnp.stack(vp), jnp.stack(sp), jnp.stack(ksm), jnp.stack(vsm), jnp.stack(ssm))
```

```python
import math
import numpy as np
import concourse.bass as bass
import concourse.mybir as mybir
from concourse.bass_utils import run_bass_kernel_spmd

F32 = mybir.dt.float32
BF16 = mybir.dt.bfloat16
AF = mybir.ActivationFunctionType
ALU = mybir.AluOpType

D = 1024
NMETA = 16
CHUNK = 64
RH, RDK, RDV = 4, 128, 256
DH_, DDH, DDV = 8, 64, 128
DFF = 2816
NKC = 8
TS = 512
NEG = -30000.0
LAM_INIT = 0.8 - 0.6 * math.exp(-0.3 * 0)
GAM = [1.0 - 2.0 ** (-5.0 - h) for h in range(RH)]
LG = [math.log(x) for x in GAM]
C_RQ, C_RK, C_RV, C_RG, C_DQ, C_DK, C_DV, C_GR, C_GD = 0, 512, 1024, 2048, 3072, 4096, 5120, 6144, 7168
C_RQS, C_RKS = 8192, 8704
WINX = 9216


class Buf:
    __slots__ = ("w", "rs", "excl")

    def __init__(self, excl=False):
        self.w = None
        self.rs = []
        self.excl = excl


class Ctx:
    NDS = 24

    def __init__(self, nc):
        self.nc = nc
        self.eng = {"pe": nc.tensor, "act": nc.scalar, "dve": nc.vector, "pool": nc.gpsimd, "sp": nc.sync}
        self.sem = {k: nc.alloc_semaphore("c_" + k) for k in ("pe", "act", "dve", "pool")}
        self.cnt = {k: 0 for k in self.sem}
        self.waited = {k: {} for k in self.eng}
        self.dsem = [nc.alloc_semaphore("d%d" % i) for i in range(self.NDS)]
        self.dcnt = [0] * self.NDS
        self.dnext = 0
        self.out_toks = []
        self.n_ins = 0

    def _wait(self, e, tok):
        if tok is None:
            return
        key, val, sem = tok
        if self.waited[e].get(key, 0) >= val:
            return
        self.waited[e][key] = val
        self.eng[e].wait_ge(sem, val)

    def _deps(self, e, r, w, pe_skip=False):
        for b in r:
            if b.w is not None and not (pe_skip and b.w[0] == "pe"):
                self._wait(e, b.w)
            if b.excl:
                for t in b.rs:
                    if t[0] != e:
                        self._wait(e, t)
        for b in w:
            if b.w is not None and not (pe_skip and b.w[0] == "pe"):
                self._wait(e, b.w)
            for t in b.rs:
                if not (pe_skip and t[0] == "pe"):
                    self._wait(e, t)

    def _mark(self, tok, r, w):
        for b in r:
            b.rs = [t for t in b.rs if t[0] != tok[0]] + [tok]
        for b in w:
            b.w = tok
            b.rs = []

    def op(self, e, fn, r=(), w=()):
        self._deps(e, r, w, pe_skip=(e == "pe"))
        ins = fn(self.eng[e])
        self.cnt[e] += 1
        ins.then_inc(self.sem[e], 1)
        tok = (e, self.cnt[e], self.sem[e])
        self._mark(tok, r, w)
        self.n_ins += 1
        return tok

    def dma(self, q, out, in_, r=(), w=(), is_out=False):
        s = self.dnext
        self.dnext = (self.dnext + 1) % self.NDS
        self._deps(q, r, w)
        if self.dcnt[s] > 0:
            self._wait(q, ("d%d" % s, 16 * self.dcnt[s], self.dsem[s]))
        self.dcnt[s] += 1
        self.eng[q].dma_start(out=out, in_=in_).then_inc(self.dsem[s], 16)
        tok = ("d%d" % s, 16 * self.dcnt[s], self.dsem[s])
        self._mark(tok, r, w)
        if is_out:
            self.out_toks.append(tok)
        self.n_ins += 1
        return tok

    def barrier(self):
        toks = [(k, self.cnt[k], self.sem[k]) for k in self.sem if self.cnt[k] > 0]
        toks += [("d%d" % s, 16 * self.dcnt[s], self.dsem[s]) for s in range(self.NDS) if self.dcnt[s] > 0]
        for e in self.eng:
            for t in toks:
                self._wait(e, t)

    def finish(self):
        for s in range(self.NDS):
            if self.dcnt[s] > 0:
                self._wait("sp", ("d%d" % s, 16 * self.dcnt[s], self.dsem[s]))
        for k in self.sem:
            if self.cnt[k] > 0:
                self._wait("sp", (k, self.cnt[k], self.sem[k]))


class Sb:
    def __init__(self, nc, cap):
        self.base = 16512
        self.nc, self.top, self.cap, self.n = nc, 0, cap, 0

    def alloc(self, shape, dt, name="t"):
        esz = 4 if dt == F32 else 2
        per = 1
        for s in shape[1:]:
            per *= s
        nbytes = (per * esz + 63) // 64 * 64
        assert self.top + nbytes <= self.cap, ("SBUF overflow", name, self.top, nbytes)
        self.n += 1
        t = self.nc.alloc_sbuf_tensor_at("%s_%d" % (name, self.n), list(shape), dt, offset=self.base + self.top)
        self.top += nbytes
        return t


def t5_bucket_np(rel):
    import jax
    import jax.numpy as jnp
    with jax.default_device(jax.devices("cpu")[0]):
        rel = jnp.asarray(np.asarray(rel, dtype=np.int32))
        nb, max_exact = 16, 8
        ret = jnp.where(rel > 0, nb, 0)
        n = jnp.abs(rel)
        nf = jnp.maximum(n, max_exact).astype(jnp.float32)
        large = max_exact + (jnp.log(nf / max_exact) / math.log(128 / max_exact) * (nb - max_exact)).astype(jnp.int32)
        large = jnp.minimum(large, nb - 1)
        return np.asarray(ret + jnp.where(n < max_exact, n, large))


def build(SEQ, P, phases=(1, 2, 3)):
    NT = SEQ // TS
    NOWN = NT // 2
    LTOK = NMETA + SEQ
    PB = P // 128
    nc = bass.Bass("TRN2", target_bir_lowering=False)
    cx = Ctx(nc)

    def din(name, shape, dt=F32):
        return nc.dram_tensor(name, list(shape), dt, kind="ExternalInput")

    def dout(name, shape, dt=F32):
        return nc.dram_tensor(name, list(shape), dt, kind="ExternalOutput")

    def dscr(name, shape, dt):
        return nc.dram_tensor(name, list(shape), dt)

    def bc_ap(t, off, n):
        return bass.AP(tensor=t, offset=off, ap=[[0, 128], [1, n]])

    x_all = din("x_all", [SEQ, D])
    x_own = din("x_own", [NOWN, TS, D])
    meta = din("meta", [NMETA, D])
    w_in = din("w_in", [D, WINX])
    w_rb = din("w_rb", [D, D])
    w_db = din("w_db", [D, D])
    w_o = din("w_o", [D, D])
    w_up = din("w_up", [D, 2 * DFF])
    w_dn = din("w_dn", [DFF, D])
    g1row = din("g1row", [1, D])
    g2row = din("g2row", [1, D])
    subg = din("subg", [128, 1])
    normf = din("normf", [1, D])
    lamv = din("lamv", [1, 4 * DDH])
    relb = din("relb", [32, DH_])
    c_ident = din("c_ident", [128, 128])
    c_j = din("c_j", [128, 128])
    c_j16 = din("c_j16", [16, 16])
    rot_tm = din("rot_tm", [LTOK, 128])
    rot_tm_s = din("rot_tm_s", [16, 128])
    kdec_t = din("kdec_t", [TS, RH])
    kdec_m = din("kdec_m", [NMETA, RH])
    rot_fm = din("rot_fm", [NOWN, 4, 128, TS])
    rot_fm_s = din("rot_fm_s", [4, 128, 32])
    dt_tab = din("dt_tab", [128, 16, TS])
    dt_s = din("dt_s", [16, RH, 16])
    qdec = din("qdec", [1, RH * TS])
    qdec_s = din("qdec_s", [1, RH * 32])
    selc = din("selc", [128, 2])
    ohr = din("ohr", [32, 1664])
    ohs = din("ohs", [32, 320])
    mask8 = din("mask8", [128, 9, TS])
    xs_in = din("xs_in", [2, 16, D])
    ck_in = din("ck_in", [2, P, D])
    cv_in = din("cv_in", [2, P, D])
    st_in = din("st_in", [2, RH, RDK, RDV])

    k_rows = dout("k_rows", [LTOK, D])
    v_rows = dout("v_rows", [LTOK, D])
    ret_state = dout("ret_state", [RH, RDK, RDV])
    y_own = dout("y_own", [NOWN, TS, D])
    y_s = dout("y_s", [32, D])
    k_rows_s = dout("k_rows_s", [32, D])
    v_rows_s = dout("v_rows_s", [32, D])
    ret_state_s = dout("ret_state_s", [2, RH, RDK, RDV])

    wi_s = dscr("wi_s", [D, WINX], BF16)
    wrb_s = dscr("wrb_s", [D, D], BF16)
    wdb_s = dscr("wdb_s", [D, D], BF16)
    wo_s = dscr("wo_s", [D, D], BF16)
    wup_s = dscr("wup_s", [D, 2 * DFF], BF16)
    wdn_s = dscr("wdn_s", [DFF, D], BF16)
    KTs = dscr("KTs", [DH_, 128, LTOK], BF16)
    Vs = dscr("Vs", [DH_, 128, SEQ // 128, DDV], BF16)
    Vms = dscr("Vms", [DH_, NMETA, DDV], BF16)
    KTn = dscr("KTn", [2, DH_, 128, 16], BF16)
    Vn = dscr("Vn", [2, DH_, 16, DDV], BF16)
    snaps = dscr("snaps", [NT + 1, RH, 128, RDV], BF16)
    snap_s = dscr("snap_s", [2, RH, 128, RDV], BF16)
    FRs = dscr("FRs", [DH_, 1664], BF16)
    FRss = dscr("FRss", [DH_, 320], BF16)
    Ts = dscr("Ts", [DH_, 128, 9, TS], BF16)
    Tms = dscr("Tms", [DH_, 16, TS], BF16)
    Tss = dscr("Tss", [DH_, 128, 16], BF16)
    Tsn = dscr("Tsn", [DH_, 16, 16], BF16)

    sb = Sb(nc, 207 * 1024)
    pairs = [nc.alloc_psum_tensor("pair%d" % i, [128, 1024], F32) for i in range(4)]
    banks = [pairs[i // 2][:, (i % 2) * 512:(i % 2 + 1) * 512] for i in range(8)]
    bankb = [Buf(excl=True) for _ in range(8)]

    ident = sb.alloc([128, 128], F32, "ident")
    identb = sb.alloc([128, 128], BF16, "identb")
    jmatb = sb.alloc([128, 128], BF16, "jmatb")
    j16b = sb.alloc([16, 16], BF16, "j16b")
    onesb = sb.alloc([128, 128], BF16, "onesb")
    subgs = sb.alloc([128, 1], F32, "subgs")
    eps6 = sb.alloc([128, 1], F32, "eps6")
    eps5 = sb.alloc([128, 1], F32, "eps5")
    zcol = sb.alloc([128, 1], F32, "zcol")
    cbias = sb.alloc([128, DH_], F32, "cbias")
    neglam = sb.alloc([128, 1], F32, "neglam")
    selcs = sb.alloc([128, 2], F32, "selcs")
    normfb = sb.alloc([128, D], F32, "normfb")
    cb = Buf()
    ctmp = sb.alloc([128, 128], F32, "ctmp")
    cx.dma("sp", ident[:, :], c_ident[:, :], w=[cb])
    cx.dma("sp", ctmp[:, :], c_j[:, :], w=[cb])
    cx.dma("sp", subgs[:, :], subg[:, :], w=[cb])
    cx.dma("sp", selcs[:, :], selc[:, :], w=[cb])
    cx.dma("sp", cbias[:, :], bc_ap(relb, 15 * DH_, DH_), w=[cb])
    cx.dma("sp", normfb[:, :], bc_ap(normf, 0, D), w=[cb])
    cx.op("dve", lambda e: e.memset(onesb[:, :], 1.0), w=[cb])
    cx.op("dve", lambda e: e.memset(eps6[:, :], 1e-6), w=[cb])
    cx.op("dve", lambda e: e.memset(eps5[:, :], 1e-5), w=[cb])
    cx.op("dve", lambda e: e.memset(zcol[:, :], 0.0), w=[cb])
    cx.op("dve", lambda e: e.tensor_copy(out=identb[:, :], in_=ident[:, :]), r=[cb], w=[cb])
    cx.op("dve", lambda e: e.tensor_copy(out=jmatb[:, :], in_=ctmp[:, :]), r=[cb], w=[cb])
    cx.op("dve", lambda e: e.tensor_scalar(out=subgs[:, :], in0=subgs[:, :], scalar1=float(1.0 - LAM_INIT), scalar2=None, op0=ALU.mult),
          r=[cb], w=[cb])
    cx.dma("sp", ctmp[0:16, 0:16], c_j16[:, :], r=[cb], w=[cb])
    cx.op("dve", lambda e: e.tensor_copy(out=j16b[:, :], in_=ctmp[0:16, 0:16]), r=[cb], w=[cb])
    lvt = sb.alloc([128, 4 * DDH], F32, "lvt")
    lpr = sb.alloc([128, 2 * DDH], F32, "lpr")
    lsc = sb.alloc([128, 4], F32, "lsc")
    cx.dma("sp", lvt[:, :], bc_ap(lamv, 0, 4 * DDH), w=[cb])
    cx.op("dve", lambda e: e.tensor_tensor(out=lpr[:, 0:64], in0=lvt[:, 0:64], in1=lvt[:, 64:128], op=ALU.mult), r=[cb], w=[cb])
    cx.op("dve", lambda e: e.tensor_tensor(out=lpr[:, 64:128], in0=lvt[:, 128:192], in1=lvt[:, 192:256], op=ALU.mult), r=[cb], w=[cb])
    cx.op("dve", lambda e: e.reduce_sum(out=lsc[:, 0:1], in_=lpr[:, 0:64], axis=mybir.AxisListType.X), r=[cb], w=[cb])
    cx.op("dve", lambda e: e.reduce_sum(out=lsc[:, 1:2], in_=lpr[:, 64:128], axis=mybir.AxisListType.X), r=[cb], w=[cb])
    cx.op("act", lambda e: e.activation(out=lsc[:, 2:4], in_=lsc[:, 0:2], func=AF.Exp, bias=zcol[:, :], scale=1.0), r=[cb], w=[cb])
    cx.op("dve", lambda e: e.tensor_tensor(out=neglam[:, :], in0=lsc[:, 3:4], in1=lsc[:, 2:3], op=ALU.subtract), r=[cb], w=[cb])
    cx.op("dve", lambda e: e.tensor_scalar(out=neglam[:, :], in0=neglam[:, :], scalar1=float(-LAM_INIT), scalar2=None, op0=ALU.add),
          r=[cb], w=[cb])

    mark0 = sb.top
    g1b = sb.alloc([128, D], F32, "g1b")
    g2b = sb.alloc([128, D], F32, "g2b")
    cx.dma("sp", g1b[:, :], bc_ap(g1row, 0, D), w=[cb])
    cx.dma("sp", g2b[:, :], bc_ap(g2row, 0, D), w=[cb])
    mark0 = sb.top
    n_once = [0]

    def cast_dma(dst_ap, src_ap):
        sem = nc.alloc_semaphore("w%d" % n_once[0])
        n_once[0] += 1
        nc.gpsimd.dma_start(out=dst_ap, in_=src_ap).then_inc(sem, 16)
        b = Buf()
        b.w = ("w%d" % n_once[0], 16, sem)
        return b

    wb = {}
    wb["in_kv"] = [cast_dma(wi_s[:, 512:2048], w_in[:, 512:2048]), cast_dma(wi_s[:, 4096:6144], w_in[:, 4096:6144])]
    wb["in_q"] = [cast_dma(wi_s[:, 0:512], w_in[:, 0:512]), cast_dma(wi_s[:, 2048:4096], w_in[:, 2048:4096]),
                  cast_dma(wi_s[:, 6144:WINX], w_in[:, 6144:WINX])]
    if 2 in phases or 3 in phases:
        wb["rb"] = [cast_dma(wrb_s[:, :], w_rb[:, :])]
        wb["db"] = [cast_dma(wdb_s[:, :], w_db[:, :])]
        wb["o"] = [cast_dma(wo_s[:, :], w_o[:, :])]
        wb["up"] = [cast_dma(wup_s[h_ * 512:(h_ + 1) * 512, :], w_up[h_ * 512:(h_ + 1) * 512, :]) for h_ in range(2)]
        wb["dn"] = [cast_dma(wdn_s[h_ * 1408:(h_ + 1) * 1408, :], w_dn[h_ * 1408:(h_ + 1) * 1408, :]) for h_ in range(2)]
    wkey = {}
    for nm_, t_ in (("in", wi_s), ("rb", wrb_s), ("db", wdb_s), ("o", wo_s), ("up", wup_s), ("dn", wdn_s)):
        wkey[t_.name] = nm_
    cast_iter = iter(())

    def load_slab(dst, dstb, wsrc, c0, ncols, nkc=NKC, k0=0, q="sp"):
        src = wsrc[k0 * 128:(k0 + nkc) * 128, c0:c0 + ncols].rearrange("(kc p) n -> p kc n", p=128)
        nm_ = wkey[wsrc.name]
        if nm_ == "in":
            deps = wb["in_kv"] if (512 <= c0 < 2048 or 4096 <= c0 < 6144) else wb["in_q"]
        else:
            deps = wb[nm_]
        return cx.dma(q, dst[:, 0:nkc, 0:ncols], src, r=deps, w=[dstb])

    def cp(eng, out, in_, r, w):
        if eng == "act":
            return cx.op("act", lambda e: e.copy(out=out, in_=in_), r=r, w=w)
        return cx.op(eng, lambda e: e.tensor_copy(out=out, in_=in_), r=r, w=w)

    def rms_rows(xt_ap, npart, dim, stat, statb, col, junk, junkb, rb, eps_ap):
        cx.op("act", lambda e: e.activation(out=junk[0:npart, 0:dim], in_=xt_ap, func=AF.Square,
                                            accum_out=stat[0:npart, col:col + 1]), r=rb, w=[junkb, statb])
        cx.op("act", lambda e: e.activation(out=stat[0:npart, 8 + col:9 + col], in_=stat[0:npart, col:col + 1], func=AF.Sqrt,
                                            bias=eps_ap[0:npart, :], scale=1.0 / dim), r=[statb, cb], w=[statb])
        cx.op("dve", lambda e: e.reciprocal(out=stat[0:npart, 16 + col:17 + col], in_=stat[0:npart, 8 + col:9 + col]),
              r=[statb], w=[statb])

    def rms_scale(xt, xtb, npart, nblk, xn_list, xnb_list, junk, junkb, stat, statb, gainb=None, soff=0):
        gb_ = g1b if gainb is None else gainb
        for blk in range(nblk):
            xn_tmp, xn_tmpb = xn_list[blk % len(xn_list)], xnb_list[blk % len(xn_list)]
            rms_rows(xt[0:npart, blk, :], npart, D, stat, statb, soff + blk, junk, junkb, [xtb], eps6)
            cx.op("dve", lambda e: e.scalar_tensor_tensor(out=xn_tmp[0:npart, :], in0=xt[0:npart, blk, :],
                                                          scalar=stat[0:npart, 16 + soff + blk:17 + soff + blk], in1=gb_[0:npart, :],
                                                          op0=ALU.mult, op1=ALU.mult), r=[xtb, statb, cb], w=[xn_tmpb])

    def rms_transp(npart, nblk, xnT, xnTb, xn_list, xnb_list, pb):
        for blk in range(nblk):
            xn_tmp, xn_tmpb = xn_list[blk % len(xn_list)], xnb_list[blk % len(xn_list)]
            for half in range(2):
                bk = pb[half]
                for kk in range(4):
                    kc = half * 4 + kk
                    cx.op("pe", lambda e: e.transpose(out=banks[bk][:, kk * 128:kk * 128 + npart],
                                                      in_=xn_tmp[0:npart, kc * 128:(kc + 1) * 128],
                                                      identity=ident[0:npart, 0:npart]),
                          r=[xn_tmpb, cb], w=[bankb[bk]])
                src = banks[bk][:, :].rearrange("p (k t) -> p k t", k=4)[:, :, 0:npart]
                dst = xnT[:, half * 4:half * 4 + 4, blk * npart:(blk + 1) * npart]
                cp("act" if half == 0 else "dve", dst, src, [bankb[bk]], [xnTb])

    def rms_to_fm(xt, xtb, npart, nblk, xnT, xnTb, xn_tmp, xn_tmpb, junk, junkb, stat, statb, pb, gainb=None):
        xn_list, xnb_list = (xn_tmp, xn_tmpb) if isinstance(xn_tmp, list) else ([xn_tmp], [xn_tmpb])
        for blk in range(nblk):
            xl, xbl = [xn_list[blk % len(xn_list)]], [xnb_list[blk % len(xn_list)]]
            gb_ = g1b if gainb is None else gainb
            rms_rows(xt[0:npart, blk, :], npart, D, stat, statb, blk, junk, junkb, [xtb], eps6)
            cx.op("dve", lambda e: e.scalar_tensor_tensor(out=xl[0][0:npart, :], in0=xt[0:npart, blk, :],
                                                          scalar=stat[0:npart, 16 + blk:17 + blk], in1=gb_[0:npart, :],
                                                          op0=ALU.mult, op1=ALU.mult), r=[xtb, statb, cb], w=[xbl[0]])
            for half in range(2):
                bk = pb[half]
                for kk in range(4):
                    kc = half * 4 + kk
                    cx.op("pe", lambda e: e.transpose(out=banks[bk][:, kk * 128:kk * 128 + npart],
                                                      in_=xl[0][0:npart, kc * 128:(kc + 1) * 128],
                                                      identity=ident[0:npart, 0:npart]),
                          r=[xbl[0], cb], w=[bankb[bk]])
                src = banks[bk][:, :].rearrange("p (k t) -> p k t", k=4)[:, :, 0:npart]
                dst = xnT[:, half * 4:half * 4 + 4, blk * npart:(blk + 1) * npart]
                cp("act" if half == 0 else "dve", dst, src, [bankb[bk]], [xnTb])

    if 1 in phases:
        mark1 = sb.top
        wres = sb.alloc([128, 7, NKC, 512], BF16, "wres")
        wresb = Buf()
        for i, c0 in enumerate([C_DK, C_DK + 512, C_DV, C_DV + 512, C_RK, C_RV, C_RV + 512]):
            load_slab(wres[:, i], wresb, wi_s, c0, 512, q="sp")
        xts = [sb.alloc([128, 4, D], F32, "xt") for _ in range(2)]
        xtb = [Buf(), Buf()]
        xnT2 = [sb.alloc([128, NKC, TS], BF16, "xnT") for _ in range(2)]
        xnT2b = [Buf(), Buf()]
        xn_tmp = [sb.alloc([128, D], F32, "xn_tmp") for _ in range(4)]
        xn_tmpb = [Buf() for _ in range(4)]
        junk = sb.alloc([128, D], BF16, "junk")
        junkb = Buf()
        stat = sb.alloc([128, 24], F32, "stat")
        statb = Buf()
        ktile = [sb.alloc([128, TS], BF16, "ktile") for _ in range(8)]
        ktb = [Buf() for _ in range(8)]
        stgf = [sb.alloc([128, D], F32, "stgf") for _ in range(4)]
        stgb = [Buf() for _ in range(4)]
        vbf = [sb.alloc([128, D], BF16, "vbf") for _ in range(4)]
        vbb = [Buf() for _ in range(4)]
        rott = [sb.alloc([128, 4, 128], F32, "rott") for _ in range(2)]
        rotb = [Buf(), Buf()]
        kdt = sb.alloc([128, 4, RH], F32, "kdt")
        kdm = sb.alloc([NMETA, RH], F32, "kdm")
        kdb = Buf()
        cx.dma("sp", kdt[:, :, :], kdec_t[:, :].rearrange("(b p) h -> p b h", p=128), w=[kdb])
        cx.dma("sp", kdm[:, :], kdec_m[:, :], w=[kdb])
        rtmp = [sb.alloc([128, RH, 64], F32, "rtmp") for _ in range(4)]
        rtb = Buf()
        krot = sb.alloc([128, RH, 128], F32, "krot")
        krotb = Buf()
        krd = sb.alloc([128, 4, RH * RDK], BF16, "krd")
        krdb = Buf()
        rvb = sb.alloc([128, 4, RH * RDV], BF16, "rvb")
        rvbb = Buf()
        S = sb.alloc([128, RH, RDV], F32, "S")
        Sb_ = Buf()
        snapt = sb.alloc([128, RH, RDV], BF16, "snapt")
        snapb = Buf()
        cx.op("dve", lambda e: e.memset(S[:, :, :], 0.0), w=[Sb_])

        tiles = [("meta", 0)] + [("frame", t) for t in range(NT)]
        if 3 in phases:
            tiles += [("samp", 0), ("samp", 1)]
        sti = 0
        deferred_su = []
        for ti, (kind, t) in enumerate(tiles):
            npart, nblk = (128, 4) if kind == "frame" else (NMETA, 1)
            N = npart * nblk
            tok0 = NMETA + t * TS if kind == "frame" else 0
            xs = ti % 2

            def load_x(ti_):
                kind_, t_ = tiles[ti_]
                xs_ = ti_ % 2
                if kind_ == "meta":
                    cx.dma("sp", xts[xs_][0:NMETA, 0, :], meta[:, :], w=[xtb[xs_]])
                elif kind_ == "samp":
                    cx.dma("sp", xts[xs_][0:16, 0, :], xs_in[t_, :, :], w=[xtb[xs_]])
                else:
                    cx.dma("sp", xts[xs_][:, :, :], x_all[t_ * TS:(t_ + 1) * TS, :].rearrange("(b p) d -> p b d", p=128), w=[xtb[xs_]])

            def load_rot(ti_):
                kind_, t_ = tiles[ti_]
                rs2 = ti_ % 2
                if kind_ == "samp":
                    cx.dma("sp", rott[rs2][0:16, 0, :], rot_tm_s[:, :], w=[rotb[rs2]])
                elif kind_ == "meta":
                    cx.dma("sp", rott[rs2][0:NMETA, 0, :], rot_tm[0:NMETA, :], w=[rotb[rs2]])
                else:
                    r0_ = NMETA + t_ * TS
                    cx.dma("sp", rott[rs2][:, :, :], rot_tm[r0_:r0_ + TS, :].rearrange("(b p) f -> p b f", p=128), w=[rotb[rs2]])

            if ti == 0:
                load_x(0)
                load_rot(0)
            if ti + 1 < len(tiles):
                load_x(ti + 1)
                load_rot(ti + 1)
            if kind == "samp":
                while len(deferred_su) > 0:
                    deferred_su.pop(0)()
                cx.dma("sp", S[:, :, :], st_in[t, :, :, :].rearrange("h p e -> p h e"), w=[Sb_])
                cx.op("act", lambda e: e.copy(out=snapt[:, :, :], in_=S[:, :, :]), r=[Sb_], w=[snapb])
                cx.dma("sp", snap_s[t, :, :, :].rearrange("h p e -> p h e"), snapt[:, :, :], r=[snapb])
            xnT, xnTb = xnT2[ti % 2], xnT2b[ti % 2]

            def tile_dims(ti_):
                return (128, 4) if tiles[ti_][0] == "frame" else (NMETA, 1)

            if ti == 0:
                rms_scale(xts[xs], xtb[xs], npart, nblk, xn_tmp, xn_tmpb, junk, junkb, stat, statb, soff=0)
                rms_transp(npart, nblk, xnT, xnTb, xn_tmp, xn_tmpb, (0, 1))
            for h in range(DH_):
                bk = 2 + (h % 2)
                for kc in range(NKC):
                    cx.op("pe", lambda e: e.matmul(banks[bk][:, 0:N], lhsT=wres[:, h // 4, kc, (h % 4) * 128:(h % 4 + 1) * 128],
                                                   rhs=xnT[:, kc, 0:N], start=(kc == 0), stop=(kc == NKC - 1)),
                          r=[wresb, xnTb], w=[bankb[bk]])
                ks = h
                cp("act" if h % 2 == 0 else "dve", ktile[ks][:, 0:N], banks[bk][:, 0:N], [bankb[bk]], [ktb[ks]])
                if kind == "samp":
                    cx.dma("sp", KTn[t, h, :, :], ktile[ks][:, 0:N], r=[ktb[ks]])
                else:
                    cx.dma("sp", KTs[h, :, tok0:tok0 + N], ktile[ks][:, 0:N], r=[ktb[ks]])
            while len(deferred_su) > 0:
                deferred_su.pop(0)()
            if ti + 1 < len(tiles):
                np1, nb1 = tile_dims(ti + 1)
                xs1 = (ti + 1) % 2
                rms_scale(xts[xs1], xtb[xs1], np1, nb1, xn_tmp, xn_tmpb, junk, junkb, stat, statb, soff=4 * ((ti + 1) % 2))
            for blk in range(nblk):
                lo = blk * npart
                r0 = tok0 + lo
                if blk == min(1, nblk - 1) and ti + 1 < len(tiles):
                    rms_transp(np1, nb1, xnT2[(ti + 1) % 2], xnT2b[(ti + 1) % 2], xn_tmp, xn_tmpb, (0, 1))

                def tm_mm(bk, slab_i):
                    for kc in range(NKC):
                        cx.op("pe", lambda e: e.matmul(banks[bk][0:npart, :], lhsT=xnT[:, kc, lo:lo + npart], rhs=wres[:, slab_i, kc, :],
                                                       start=(kc == 0), stop=(kc == NKC - 1)), r=[wresb, xnTb], w=[bankb[bk]])
                kdst = k_rows_s[t * 16:(t + 1) * 16, :] if kind == "samp" else k_rows[r0:r0 + npart, :]
                vdst = v_rows_s[t * 16:(t + 1) * 16, :] if kind == "samp" else v_rows[r0:r0 + npart, :]
                s = sti % 4
                sti += 1
                for half in range(2):
                    bk = 4 + half
                    tm_mm(bk, half)
                    cp("act" if half == 0 else "dve", stgf[s][0:npart, half * 512:(half + 1) * 512], banks[bk][0:npart, :],
                       [bankb[bk]], [stgb[s]])
                cx.dma("sp", kdst, stgf[s][0:npart, :], r=[stgb[s]], is_out=True)
                s = sti % 4
                sti += 1
                vs = blk % 4
                for half in range(2):
                    bk = 6 + half
                    tm_mm(bk, 2 + half)
                    cp("act" if half == 0 else "dve", stgf[s][0:npart, half * 512:(half + 1) * 512], banks[bk][0:npart, :],
                       [bankb[bk]], [stgb[s]])
                cx.op("act", lambda e: e.copy(out=vbf[vs][0:npart, :], in_=stgf[s][0:npart, :]), r=[stgb[s]], w=[vbb[vs]])
                cx.dma("sp", vdst, stgf[s][0:npart, :], r=[stgb[s]], is_out=True)
                vsrc = vbf[vs][0:npart, :].rearrange("p (h e) -> p h e", h=DH_)
                if kind == "meta":
                    cx.dma("sp", Vms[:, :, :].rearrange("h p e -> p h e"), vsrc, r=[vbb[vs]])
                elif kind == "samp":
                    cx.dma("sp", Vn[t, :, :, :].rearrange("h p e -> p h e"), vsrc, r=[vbb[vs]])
                else:
                    cx.dma("sp", Vs[:, :, t * 4 + blk, :].rearrange("h p e -> p h e"), vsrc, r=[vbb[vs]])
                rs_ = ti % 2
                bk = blk % 2
                tm_mm(bk, 4)
                for hh in range(RH):
                    cosr = rott[rs_][0:npart, blk, 0:64]
                    sinr = rott[rs_][0:npart, blk, 64:128]
                    x1 = banks[bk][0:npart, hh * 128:hh * 128 + 64]
                    x2 = banks[bk][0:npart, hh * 128 + 64:hh * 128 + 128]
                    for ri, (xa, tb_) in enumerate([(x1, cosr), (x2, sinr), (x1, sinr), (x2, cosr)]):
                        cx.op("dve", lambda e: e.tensor_tensor(out=rtmp[ri][0:npart, hh, :], in0=xa, in1=tb_, op=ALU.mult),
                              r=[bankb[bk], rotb[rs_]], w=[rtb])
                cx.op("dve", lambda e: e.tensor_tensor(out=krot[0:npart, :, 0:64], in0=rtmp[0][0:npart, :, :], in1=rtmp[1][0:npart, :, :],
                                                       op=ALU.subtract), r=[rtb], w=[krotb])
                cx.op("dve", lambda e: e.tensor_tensor(out=krot[0:npart, :, 64:128], in0=rtmp[2][0:npart, :, :], in1=rtmp[3][0:npart, :, :],
                                                       op=ALU.add), r=[rtb], w=[krotb])
                for hh in range(RH):
                    kd = kdt[:, blk, hh:hh + 1] if kind == "frame" else kdm[0:NMETA, hh:hh + 1]
                    cx.op("act", lambda e: e.activation(out=krd[0:npart, blk, hh * 128:(hh + 1) * 128], in_=krot[0:npart, hh, :],
                                                        func=AF.Copy, scale=kd), r=[krotb, kdb], w=[krdb])
                for half in range(2):
                    bk = 6 + half
                    tm_mm(bk, 5 + half)
                    cp("act" if half == 0 else "dve", rvb[0:npart, blk, half * 512:(half + 1) * 512], banks[bk][0:npart, :],
                       [bankb[bk]], [rvbb])
            def state_update(kind=kind, t=t, npart=npart, nblk=nblk, N=N):
                for hh in range(RH):
                    bk = 2 + (hh % 2)
                    for blk in range(nblk):
                        cx.op("pe", lambda e: e.matmul(banks[bk][:, 0:RDV], lhsT=krd[0:npart, blk, hh * 128:(hh + 1) * 128],
                                                       rhs=rvb[0:npart, blk, hh * RDV:(hh + 1) * RDV], start=(blk == 0), stop=(blk == nblk - 1)),
                              r=[krdb, rvbb], w=[bankb[bk]])
                    gN = GAM[hh] ** N
                    cx.op("dve", lambda e: e.scalar_tensor_tensor(out=S[:, hh, :], in0=S[:, hh, :], scalar=float(gN), in1=banks[bk][:, 0:RDV],
                                                                  op0=ALU.mult, op1=ALU.add), r=[bankb[bk], Sb_], w=[Sb_])
                if kind == "samp":
                    cx.dma("sp", ret_state_s[t, :, :, :].rearrange("h p e -> p h e"), S[:, :, :], r=[Sb_], is_out=True)
                else:
                    cx.op("act", lambda e: e.copy(out=snapt[:, :, :], in_=S[:, :, :]), r=[Sb_], w=[snapb])
                    sidx = 0 if kind == "meta" else t + 1
                    cx.dma("sp", snaps[sidx, :, :, :].rearrange("h p e -> p h e"), snapt[:, :, :], r=[snapb])
                    if kind == "frame" and t == NT - 1:
                        cx.dma("sp", ret_state[:, :, :].rearrange("h p e -> p h e"), S[:, :, :], r=[Sb_], is_out=True)

            deferred_su.append(state_update)
        while len(deferred_su) > 0:
            deferred_su.pop(0)()
        cx.barrier()
        sb.top = mark0

    if 2 in phases or 3 in phases:
        markS = sb.top
        relf = sb.alloc([32, DH_], F32, "relf")
        relbb = sb.alloc([32, DH_], BF16, "relbb")
        ohf = sb.alloc([32, 1664], F32, "ohf")
        ohb = sb.alloc([32, 1664], BF16, "ohb")
        frt = sb.alloc([DH_, 1664], BF16, "frt")
        sb_ = Buf()
        cx.dma("sp", relf[:, :], relb[:, :], w=[sb_])
        cx.op("dve", lambda e: e.tensor_copy(out=relbb[:, :], in_=relf[:, :]), r=[sb_], w=[sb_])
        for (src, dst, n) in ((ohr, FRs, 1664), (ohs, FRss, 320)):
            cx.dma("sp", ohf[:, 0:n], src[:, :], r=[sb_], w=[sb_])
            cx.op("dve", lambda e: e.tensor_copy(out=ohb[:, 0:n], in_=ohf[:, 0:n]), r=[sb_], w=[sb_])
            for c0 in range(0, n, 416):
                cw = min(416, n - c0)
                cx.op("pe", lambda e: e.matmul(banks[0][0:DH_, 0:cw], lhsT=relbb[:, :], rhs=ohb[:, c0:c0 + cw], start=True, stop=True),
                      r=[sb_], w=[bankb[0]])
                cx.op("dve", lambda e: e.tensor_copy(out=frt[:, c0:c0 + cw], in_=banks[0][0:DH_, 0:cw]), r=[bankb[0]], w=[sb_])
            cx.dma("sp", dst[:, :], frt[:, 0:n], r=[sb_], w=[sb_])
        msk = sb.alloc([128, 9, TS], F32, "msk")
        mskb = Buf()
        cx.dma("sp", msk[:, :, :], mask8[:, :, :], w=[mskb])
        Ht = [sb.alloc([128, TS], BF16, "Ht") for _ in range(2)]
        Htb = [Buf(), Buf()]
        Tt = [sb.alloc([128, TS], BF16, "Tt") for _ in range(2)]
        Ttb = [Buf(), Buf()]
        it = 0

        def hank(t, off, npart, n):
            return bass.AP(tensor=t, offset=off, ap=[[1, npart], [1, n]])

        for h in range(DH_):
            jobs = []
            if 2 in phases:
                jobs += [("w", p) for p in range(9)] + [("m", 0)]
            if 3 in phases:
                jobs += [("s1", 0), ("s2", 0)]
            for (kind, p) in jobs:
                s = it % 2
                bk = it % 2
                it += 1
                if kind == "w":
                    npk, n, src, jm = 128, TS, hank(FRs, h * 1664 + 1024 - 128 * p, 128, TS), jmatb[:, :]
                elif kind == "m":
                    npk, n, src, jm = 16, TS, hank(FRs, h * 1664 + 1024, 16, TS), j16b[:, :]
                elif kind == "s1":
                    npk, n, src, jm = 128, 16, hank(FRss, h * 320 + 160, 128, 16), jmatb[:, :]
                else:
                    npk, n, src, jm = 16, 16, hank(FRss, h * 320 + 144, 16, 16), j16b[:, :]
                cx.dma("sp", Ht[s][0:npk, 0:n], src, r=[sb_], w=[Htb[s]])
                cx.op("pe", lambda e: e.matmul(banks[bk][0:npk, 0:n], lhsT=jm, rhs=Ht[s][0:npk, 0:n], start=True, stop=True),
                      r=[Htb[s], cb], w=[bankb[bk]])
                if kind == "w":
                    cx.op("dve", lambda e: e.scalar_tensor_tensor(out=Tt[s][:, :], in0=banks[bk][:, :], scalar=8.0, in1=msk[:, p, :],
                                                                  op0=ALU.mult, op1=ALU.add), r=[bankb[bk], mskb], w=[Ttb[s]])
                    cx.dma("sp", Ts[h, :, p, :], Tt[s][:, :], r=[Ttb[s]])
                else:
                    cx.op("dve", lambda e: e.tensor_scalar(out=Tt[s][0:npk, 0:n], in0=banks[bk][0:npk, 0:n], scalar1=8.0, scalar2=None,
                                                           op0=ALU.mult), r=[bankb[bk]], w=[Ttb[s]])
                    dst = {"m": Tms[h, :, :], "s1": Tss[h, :, :], "s2": Tsn[h, :, :]}[kind]
                    cx.dma("sp", dst, Tt[s][0:npk, 0:n], r=[Ttb[s]])
        cx.barrier()
        sb.top = markS

        xnT = sb.alloc([128, NKC, TS], BF16, "xnT")
        xnTb = Buf()
        QT = sb.alloc([128, DH_, TS], BF16, "QT")
        QTb = Buf()
        yretT = sb.alloc([128, 8, TS], BF16, "yretT")
        yretb = Buf()
        ydiffT = sb.alloc([128, 8, TS], BF16, "ydiffT")
        ydiffb = Buf()
        xn_tmp = sb.alloc([128, D], F32, "xn_tmp")
        xn_tmpb = Buf()
        junk = sb.alloc([128, D], BF16, "junk")
        junkb = Buf()
        stat = sb.alloc([128, 24], F32, "stat")
        statb = Buf()
        rrow = sb.alloc([64, 4, TS], F32, "rrow")
        rrowb = Buf()
        bchi = sb.alloc([64, TS], BF16, "bchi")
        bclo = sb.alloc([64, TS], BF16, "bclo")
        bcb = Buf()
        xtA = sb.alloc([128, 4, D], F32, "xtA")
        xtAb = Buf()
        markQ = sb.top
        bank_ctr = [0]

        def nbank(lo=0, hi=8):
            b = lo + bank_ctr[0] % (hi - lo)
            bank_ctr[0] += 1
            return b

        def bcast_prep(row, rb, n, p0=0):
            hi, lo = bchi[p0:p0 + 1, 0:n], bclo[p0:p0 + 1, 0:n]
            cx.op("dve", lambda e: e.tensor_copy(out=hi, in_=row), r=rb, w=[bcb])
            cx.op("dve", lambda e: e.tensor_tensor(out=lo, in0=row, in1=hi, op=ALU.subtract), r=rb + [bcb], w=[bcb])

        def bcast_pe(n, bank, p0=0):
            hi, lo = bchi[p0:p0 + 1, 0:n], bclo[p0:p0 + 1, 0:n]
            cx.op("pe", lambda e: e.matmul(banks[bank][:, 0:n], lhsT=onesb[p0:p0 + 1, 0:128], rhs=hi, start=True, stop=False),
                  r=[bcb, cb], w=[bankb[bank]])
            cx.op("pe", lambda e: e.matmul(banks[bank][:, 0:n], lhsT=onesb[p0:p0 + 1, 0:128], rhs=lo, start=False, stop=True),
                  r=[bcb, cb], w=[bankb[bank]])

        def bcast_row(row, rb, n, bank, p0=0):
            bcast_prep(row, rb, n, p0)
            bcast_pe(n, bank, p0)

        class Ring:
            def __init__(self, shape, dt, n, name):
                self.t = [sb.alloc(shape, dt, name) for _ in range(n)]
                self.b = [Buf() for _ in range(n)]
                self.i = 0

            def nxt(self):
                k = self.i % len(self.t)
                self.i += 1
                return self.t[k], self.b[k]

        def fm_chunk(slab, slb, oc, N, bank):
            for kc in range(NKC):
                cx.op("pe", lambda e: e.matmul(banks[bank][:, 0:N], lhsT=slab[:, kc, oc * 128:(oc + 1) * 128], rhs=xnT[:, kc, 0:N],
                                               start=(kc == 0), stop=(kc == NKC - 1)), r=[slb, xnTb], w=[bankb[bank]])

        def ret_unit(h, Qr, Qrd, Kr, qkb, n, kbs, Sin, Sinb, srg_aps, srgb, out_aps, outb, T_, bs=0):
            ob0 = 2 * bs
            sb0 = 4 + 2 * bs
            sds = []
            for i, kb in enumerate(kbs):
                bk = sb0 + i % 2
                npk, k0, q0 = kb["npk"], kb["k0"], kb["q0"]
                cx.op("pe", lambda e: e.matmul(banks[bk][0:npk, 0:n - q0], lhsT=Kr[:, k0:k0 + npk], rhs=Qr[:, q0:n], start=True, stop=True),
                      r=[qkb], w=[bankb[bk]])
                sd, sdb = T_["sD"][bs].t[i], T_["sD"][bs].b[i]
                cx.op("dve", lambda e: e.tensor_tensor(out=sd[0:npk, 0:n - q0], in0=banks[bk][0:npk, 0:n - q0], in1=kb["dt"], op=ALU.mult),
                      r=[bankb[bk], kb["dtb"]], w=[sdb])
                sds.append((sd, sdb))
            yield
            for dvc in range(2):
                cx.op("pe", lambda e: e.matmul(banks[ob0 + dvc][:, 0:n], lhsT=Sin[:, dvc * 128:(dvc + 1) * 128], rhs=Qrd[:, 0:n],
                                               start=True, stop=False), r=[Sinb, qkb], w=[bankb[ob0 + dvc]])
                for i, kb in enumerate(kbs):
                    npk, q0 = kb["npk"], kb["q0"]
                    cx.op("pe", lambda e: e.matmul(banks[ob0 + dvc][:, q0:n], lhsT=kb["v"][:, dvc * 128:(dvc + 1) * 128],
                                                   rhs=sds[i][0][0:npk, 0:n - q0], start=False, stop=(i == len(kbs) - 1)),
                          r=[kb["vb"], sds[i][1]], w=[bankb[ob0 + dvc]])
            yield
            sqs = []
            for dvc in range(2):
                sq, sqb = T_["sq"][bs].nxt()
                cx.op("act", lambda e: e.activation(out=sq[:, 0:n], in_=banks[ob0 + dvc][:, 0:n], func=AF.Square), r=[bankb[ob0 + dvc]], w=[sqb])
                sqs.append((sq, sqb))
            yield
            for dvc in range(2):
                sq, sqb = sqs[dvc]
                cx.op("pe", lambda e: e.matmul(banks[sb0][0:1, 0:n], lhsT=onesb[:, 0:1], rhs=sq[:, 0:n], start=(dvc == 0), stop=(dvc == 1)),
                      r=[sqb, cb], w=[bankb[sb0]])
            yield
            rr = T_["rr"][bs]
            rrb = T_["rrb"][bs]
            cx.op("act", lambda e: e.activation(out=rr[0:1, 0, 0:n], in_=banks[sb0][0:1, 0:n], func=AF.Sqrt, bias=eps6[0:1, :],
                                                scale=1.0 / RDV), r=[bankb[sb0], cb], w=[rrb])
            cx.op("dve", lambda e: e.reciprocal(out=rr[0:1, 1, 0:n], in_=rr[0:1, 0, 0:n]), r=[rrb], w=[rrb])
            hi, lo = rr[0:1, 2, 0:n], rr[0:1, 3, 0:n]
            hib, lob = T_["rh"][bs][0:1, 0, 0:n], T_["rh"][bs][0:1, 1, 0:n]
            cx.op("dve", lambda e: e.tensor_copy(out=hib, in_=rr[0:1, 1, 0:n]), r=[rrb], w=[rrb])
            cx.op("dve", lambda e: e.tensor_tensor(out=lob, in0=rr[0:1, 1, 0:n], in1=hib, op=ALU.subtract), r=[rrb], w=[rrb])
            yield
            cx.op("pe", lambda e: e.matmul(banks[sb0 + 1][:, 0:n], lhsT=onesb[0:1, 0:128], rhs=hib, start=True, stop=False),
                  r=[rrb, cb], w=[bankb[sb0 + 1]])
            cx.op("pe", lambda e: e.matmul(banks[sb0 + 1][:, 0:n], lhsT=onesb[0:1, 0:128], rhs=lob, start=False, stop=True),
                  r=[rrb, cb], w=[bankb[sb0 + 1]])
            yield
            Rb, Rbb = T_["f32"][bs].nxt()
            cx.op("act", lambda e: e.copy(out=Rb[:, 0:n], in_=banks[sb0 + 1][:, 0:n]), r=[bankb[sb0 + 1]], w=[Rbb])
            yield
            for dvc in range(2):
                tmp, tmpb = T_["f32"][bs].nxt()
                cx.op("dve", lambda e: e.tensor_tensor(out=tmp[:, 0:n], in0=banks[ob0 + dvc][:, 0:n], in1=Rb[:, 0:n], op=ALU.mult),
                      r=[bankb[ob0 + dvc], Rbb], w=[tmpb])
                cx.op("pool", lambda e: e.tensor_tensor(out=out_aps[dvc], in0=tmp[:, 0:n], in1=srg_aps[dvc], op=ALU.mult),
                      r=[tmpb, srgb], w=[outb])

        def run_interleaved(makers):
            active = {}
            pend = list(makers)
            for bs in (0, 1):
                if pend:
                    active[bs] = pend.pop(0)(bs)
            while active:
                for bs in list(active.keys()):
                    try:
                        next(active[bs])
                    except StopIteration:
                        if pend:
                            active[bs] = pend.pop(0)(bs)
                        else:
                            del active[bs]

        def ret_bufs(T_, n):
            T_["sD"] = [Ring([128, n], BF16, 4, "sD") for _ in range(2)]
            T_["sq"] = [Ring([128, n], BF16, 2, "sq") for _ in range(2)]
            T_["f32"] = [Ring([128, n], F32, 3, "rf32") for _ in range(2)]
            T_["rr"] = [sb.alloc([1, 4, n], F32, "rr") for _ in range(2)]
            T_["rrb"] = [Buf(), Buf()]
            T_["rh"] = [sb.alloc([1, 2, n], BF16, "rh") for _ in range(2)]

        pending = []

        def run_pending(nmax=None):
            k = 0
            while pending and (nmax is None or k < nmax):
                pending.pop(0)()
                k += 1

        def attn_unit(qT, qTb, n, blocks, out_ap, outb, T_):
            nb = len(blocks)
            pts = {}
            use_zacc = False
            nz_last = nb - 5
            if use_zacc:
                zacc, zab, zh, zl, zhb = T_["zacc"], T_["zab"], T_["zh"], T_["zl"], T_["zhb"]
                cx.op("dve", lambda e: e.memset(zacc[0][:, 0:n], 0.0), w=[zab[0]])
                cx.op("pool", lambda e: e.memset(zacc[1][:, 0:n], 0.0), w=[zab[1]])

            def emit_st(bi):
                bl = blocks[bi]
                npk = bl["npk"]
                for c in range(2):
                    bk = 4 + (bi % 2) * 2 + c
                    cx.op("pe", lambda e: e.matmul(banks[bk][0:npk, 0:n], lhsT=bl["kT"][c * 64:(c + 1) * 64, :],
                                                   rhs=qT[c * 64:(c + 1) * 64, 0:n], start=True, stop=(bl["T"] is None)),
                          r=[bl["kb"], qTb], w=[bankb[bk]])
                for c in range(2):
                    bk = 4 + (bi % 2) * 2 + c
                    if bl["T"] is not None:
                        cx.op("pe", lambda e: e.matmul(banks[bk][0:npk, 0:n], lhsT=identb[0:npk, 0:npk], rhs=bl["T"], start=False, stop=True),
                              r=[bl["Tb"], cb], w=[bankb[bk]])
                pr = 2 + (bi % 2)
                pt2, ptb = T_["PT"].nxt()
                src2 = pairs[pr][0:npk, :].rearrange("p (c m) -> p c m", c=2)[:, :, 0:n]
                cx.op("act", lambda e: e.activation(out=pt2[0:npk, :, 0:n], in_=src2, func=AF.Exp, bias=bl["bias"][0:npk, :], scale=0.125),
                      r=[bankb[2 * pr], bankb[2 * pr + 1], cb], w=[ptb])
                for c in range(2):
                    pts[(bi, c)] = (pt2[:, c, :], ptb)

            def emit_av(bi):
                bl = blocks[bi]
                npk = bl["npk"]
                pp = [pts.pop((bi, c)) for c in range(2)]
                for c in range(2):
                    pt, ptb = pp[c]
                    cx.op("pe", lambda e: e.matmul(banks[c][:, 0:n], lhsT=bl["v"], rhs=pt[0:npk, 0:n], start=(bi == 0), stop=(bi == nb - 1)),
                          r=[bl["vb"], ptb], w=[bankb[c]])
                if bi % 2 == 0 or bi >= nz_last or not use_zacc:
                    for c in range(2):
                        pt, ptb = pp[c]
                        cx.op("pe", lambda e: e.matmul(banks[2][32 * c:32 * c + 1, 0:n], lhsT=onesb[0:npk, 0:1], rhs=pt[0:npk, 0:n],
                                                       start=(bi == 0), stop=(bi == nb - 1 and not use_zacc), tile_position=(0, 32 * c)),
                              r=[ptb, cb], w=[bankb[2]])
                else:
                    for c in range(2):
                        pt, ptb = pp[c]
                        cx.op("dve" if c == 0 else "pool",
                              lambda e: e.tensor_tensor(out=zacc[c][:, 0:n], in0=zacc[c][:, 0:n], in1=pt[:, 0:n], op=ALU.add),
                              r=[ptb, zab[c]], w=[zab[c]])
                if use_zacc and bi == nz_last:
                    for c in range(2):
                        ze = "dve" if c == 0 else "pool"
                        cx.op(ze, lambda e: e.tensor_copy(out=zh[c][:, 0:n], in_=zacc[c][:, 0:n]), r=[zab[c]], w=[zhb[c]])
                        cx.op(ze, lambda e: e.tensor_tensor(out=zl[c][:, 0:n], in0=zacc[c][:, 0:n], in1=zh[c][:, 0:n], op=ALU.subtract),
                              r=[zab[c], zhb[c]], w=[zhb[c]])
                if use_zacc and bi == nb - 1:
                    for zi, zt in enumerate((zh, zl)):
                        for c in range(2):
                            cx.op("pe", lambda e: e.matmul(banks[2][32 * c:32 * c + 1, 0:n], lhsT=onesb[:, 0:1], rhs=zt[c][:, 0:n],
                                                           start=False, stop=(zi == 1), tile_position=(0, 32 * c)),
                                  r=[zhb[c], cb], w=[bankb[2]])

            for bi in range(nb + 1):
                if bi < nb:
                    emit_st(bi)
                if bi >= 1:
                    emit_av(bi - 1)
                if bi >= 2 and (nb < 28 or bi % 2 == 0):
                    run_pending(1)
            run_pending()
            o0s, o1s, m0, a_ = T_["o0s"], T_["o1s"], T_["m0"], T_["a"]
            ob = T_["ob"]
            cx.op("dve", lambda e: e.tensor_copy(out=o0s[:, 0:n], in_=banks[0][:, 0:n]), r=[bankb[0]], w=[ob])
            cx.op("dve", lambda e: e.tensor_copy(out=o1s[:, 0:n], in_=banks[1][:, 0:n]), r=[bankb[1]], w=[ob])
            cx.op("dve", lambda e: e.tensor_copy(out=rrow[0:33, 2, 0:n], in_=banks[2][0:33, 0:n]), r=[bankb[2]], w=[rrowb])
            sq, sqb = T_["sq"].nxt()

            def s0c():
                cx.op("dve", lambda e: e.reciprocal(out=rrow[0:1, 0, 0:n], in_=rrow[0:1, 2, 0:n]), r=[rrowb], w=[rrowb])
                cx.op("dve", lambda e: e.reciprocal(out=rrow[32:33, 1, 0:n], in_=rrow[32:33, 2, 0:n]), r=[rrowb], w=[rrowb])
                cx.op("dve", lambda e: e.tensor_scalar(out=rrow[32:33, 1, 0:n], in0=rrow[32:33, 1, 0:n], scalar1=neglam[32:33, 0:1],
                                                       scalar2=None, op0=ALU.mult), r=[rrowb, cb], w=[rrowb])

            def s1():
                bcast_prep(rrow[0:1, 0, 0:n], [rrowb], n)

            def s1b():
                bcast_pe(n, 3)

            def s2():
                cx.op("dve", lambda e: e.tensor_tensor(out=m0[:, 0:n], in0=o0s[:, 0:n], in1=banks[3][:, 0:n], op=ALU.mult),
                      r=[ob, bankb[3]], w=[ob])

            def s3():
                bcast_prep(rrow[32:33, 1, 0:n], [rrowb], n, p0=32)

            def s3b():
                bcast_pe(n, 3, p0=32)

            def s4():
                cx.op("dve", lambda e: e.tensor_tensor(out=a_[:, 0:n], in0=o1s[:, 0:n], in1=banks[3][:, 0:n], op=ALU.mult),
                      r=[ob, bankb[3]], w=[ob])
                cx.op("pool", lambda e: e.tensor_tensor(out=a_[:, 0:n], in0=a_[:, 0:n], in1=m0[:, 0:n], op=ALU.add), r=[ob], w=[ob])
                cx.op("pool", lambda e: e.tensor_tensor(out=sq[:, 0:n], in0=a_[:, 0:n], in1=a_[:, 0:n], op=ALU.mult), r=[ob], w=[sqb])

            def s5():
                cx.op("pe", lambda e: e.matmul(banks[3][0:1, 0:n], lhsT=onesb[:, 0:1], rhs=sq[:, 0:n], start=True, stop=True),
                      r=[sqb, cb], w=[bankb[3]])

            def s6():
                cx.op("act", lambda e: e.activation(out=rrow[0:1, 3, 0:n], in_=banks[3][0:1, 0:n], func=AF.Ln, bias=eps5[0:1, :],
                                                    scale=1.0 / DDV), r=[bankb[3], cb], w=[rrowb])
                cx.op("act", lambda e: e.activation(out=rrow[0:1, 3, 0:n], in_=rrow[0:1, 3, 0:n], func=AF.Exp, bias=zcol[0:1, :],
                                                    scale=-0.5), r=[rrowb, cb], w=[rrowb])

            def s7():
                bcast_prep(rrow[0:1, 3, 0:n], [rrowb], n)

            def s7b():
                bcast_pe(n, 3)

            def s8():
                cx.op("dve", lambda e: e.scalar_tensor_tensor(out=out_ap, in0=a_[:, 0:n], scalar=subgs[:, 0:1], in1=banks[3][:, 0:n],
                                                              op0=ALU.mult, op1=ALU.mult), r=[ob, bankb[3], cb], w=[outb])

            pending.extend([s0c, s1, s1b, s2, s3, s3b, s4, s5, s6, s7, s7b, s8])

        def tail(N, tmb, xsrc_dma, out_dst, xt_given=None, after_wo=None):
            mk = sb.top
            nb_ = len(tmb)
            if xt_given is None:
                xt = sb.alloc([128, nb_, D], F32, "t_xt")
                xtb = Buf()
                xsrc_dma(xt, xtb)
            else:
                xt, xtb = xt_given
            slabs = Ring([128, NKC, 512], BF16, 4, "slab")
            sgr = sb.alloc([128, 8, N], BF16, "sgr")
            sgd = sb.alloc([128, 8, N], BF16, "sgd")
            sgb = Buf()
            mergedT = sb.alloc([128, 8, N], BF16, "mergedT")
            mgb = Buf()
            hh_ = sb.alloc([128, nb_, D], F32, "t_h")
            hb = Buf()
            hnT = sb.alloc([128, NKC, N], BF16, "hnT")
            hnTb = Buf()
            actT = sb.alloc([128, 22, N], BF16, "actT")
            actb = Buf()
            f32r = Ring([128, N], F32, 3, "t_f32")
            yst = Ring([128, D], F32, 2, "yst")
            for gi, (c0, dstg) in enumerate(((C_GR, sgr), (C_GR + 512, sgr), (C_GD, sgd), (C_GD + 512, sgd))):
                sl, slb = slabs.nxt()
                load_slab(sl, slb, wi_s, c0, 512)
                for oc in range(4):
                    bk = nbank()
                    fm_chunk(sl, slb, oc, N, bk)
                    cx.op("act", lambda e: e.activation(out=dstg[:, (gi % 2) * 4 + oc, 0:N], in_=banks[bk][:, 0:N], func=AF.Sigmoid,
                                                        bias=zcol[:, :], scale=1.0), r=[bankb[bk], cb], w=[sgb])
            for half in range(2):
                sr, srb = slabs.nxt()
                load_slab(sr, srb, wrb_s, half * 512, 512)
                sd_, sdb_ = slabs.nxt()
                load_slab(sd_, sdb_, wdb_s, half * 512, 512)
                for oc in range(4):
                    dc = half * 4 + oc
                    ba, bb_ = nbank(), nbank()
                    for fc in range(8):
                        cx.op("pe", lambda e: e.matmul(banks[ba][:, 0:N], lhsT=sr[:, fc, oc * 128:(oc + 1) * 128], rhs=yretT[:, fc, 0:N],
                                                       start=(fc == 0), stop=(fc == 7)), r=[srb, yretb], w=[bankb[ba]])
                    for fc in range(8):
                        cx.op("pe", lambda e: e.matmul(banks[bb_][:, 0:N], lhsT=sd_[:, fc, oc * 128:(oc + 1) * 128], rhs=ydiffT[:, fc, 0:N],
                                                       start=(fc == 0), stop=(fc == 7)), r=[sdb_, ydiffb], w=[bankb[bb_]])
                    m0, m0b = f32r.nxt()
                    cx.op("dve", lambda e: e.tensor_tensor(out=m0[:, 0:N], in0=banks[ba][:, 0:N], in1=sgr[:, dc, 0:N], op=ALU.mult),
                          r=[bankb[ba], sgb], w=[m0b])
                    m1, m1b = f32r.nxt()
                    cx.op("dve", lambda e: e.tensor_tensor(out=m1[:, 0:N], in0=banks[bb_][:, 0:N], in1=sgd[:, dc, 0:N], op=ALU.mult),
                          r=[bankb[bb_], sgb], w=[m1b])
                    cx.op("pool", lambda e: e.tensor_tensor(out=mergedT[:, dc, 0:N], in0=m0[:, 0:N], in1=m1[:, 0:N], op=ALU.add),
                          r=[m0b, m1b], w=[mgb])
            for cg in range(2):
                sl, slb = slabs.nxt()
                load_slab(sl, slb, wo_s, cg * 512, 512)
                for bi, (npart, col0) in enumerate(tmb):
                    bk = nbank()
                    for kc in range(NKC):
                        cx.op("pe", lambda e: e.matmul(banks[bk][0:npart, :], lhsT=mergedT[:, kc, col0:col0 + npart], rhs=sl[:, kc, :],
                                                       start=(kc == 0), stop=(kc == NKC - 1)), r=[slb, mgb], w=[bankb[bk]])
                    cx.op("dve", lambda e: e.tensor_tensor(out=hh_[0:npart, bi, cg * 512:(cg + 1) * 512],
                                                           in0=banks[bk][0:npart, :], in1=xt[0:npart, bi, cg * 512:(cg + 1) * 512], op=ALU.add),
                          r=[bankb[bk], xtb], w=[hb])
            if after_wo is not None:
                after_wo()
            npart0 = tmb[0][0]
            rms_to_fm(hh_, hb, npart0, nb_, hnT, hnTb, xn_tmp, xn_tmpb, junk, junkb, stat, statb, (nbank(0, 4), nbank(4, 8)), gainb=g2b)
            for i4 in range(6):
                ncol = min(512, DFF - i4 * 512)
                sg_, sgb_ = slabs.nxt()
                load_slab(sg_, sgb_, wup_s, i4 * 512, ncol)
                su_, sub_ = slabs.nxt()
                load_slab(su_, sub_, wup_s, DFF + i4 * 512, ncol)
                for oc in range(ncol // 128):
                    i = i4 * 4 + oc
                    bg, bu = nbank(), nbank()
                    for kc in range(NKC):
                        cx.op("pe", lambda e: e.matmul(banks[bg][:, 0:N], lhsT=sg_[:, kc, oc * 128:(oc + 1) * 128], rhs=hnT[:, kc, 0:N],
                                                       start=(kc == 0), stop=(kc == NKC - 1)), r=[sgb_, hnTb], w=[bankb[bg]])
                    for kc in range(NKC):
                        cx.op("pe", lambda e: e.matmul(banks[bu][:, 0:N], lhsT=su_[:, kc, oc * 128:(oc + 1) * 128], rhs=hnT[:, kc, 0:N],
                                                       start=(kc == 0), stop=(kc == NKC - 1)), r=[sub_, hnTb], w=[bankb[bu]])
                    sg_t, sgtb = f32r.nxt()
                    cx.op("act", lambda e: e.activation(out=sg_t[:, 0:N], in_=banks[bg][:, 0:N], func=AF.Silu, bias=zcol[:, :], scale=1.0),
                          r=[bankb[bg], cb], w=[sgtb])
                    cx.op("dve", lambda e: e.tensor_tensor(out=actT[:, i, 0:N], in0=sg_t[:, 0:N], in1=banks[bu][:, 0:N], op=ALU.mult),
                          r=[sgtb, bankb[bu]], w=[actb])
            for cg in range(2):
                bks = [nbank(0, 4) if False else (bi % 4) for bi in range(nb_)]
                bks = [(cg * 4 + bi) % 8 for bi in range(nb_)]
                for kg in range(3):
                    nk_ = 8 if kg < 2 else 6
                    sl, slb = slabs.nxt()
                    load_slab(sl, slb, wdn_s, cg * 512, 512, nkc=nk_, k0=kg * 8)
                    for bi, (npart, col0) in enumerate(tmb):
                        for kk in range(nk_):
                            i = kg * 8 + kk
                            cx.op("pe", lambda e: e.matmul(banks[bks[bi]][0:npart, :], lhsT=actT[:, i, col0:col0 + npart], rhs=sl[:, kk, :],
                                                           start=(i == 0), stop=(i == 21)), r=[slb, actb], w=[bankb[bks[bi]]])
                for bi, (npart, col0) in enumerate(tmb):
                    cx.op("dve", lambda e: e.tensor_tensor(out=hh_[0:npart, bi, cg * 512:(cg + 1) * 512], in0=banks[bks[bi]][0:npart, :],
                                                           in1=hh_[0:npart, bi, cg * 512:(cg + 1) * 512], op=ALU.add),
                          r=[bankb[bks[bi]], hb], w=[hb])
            for bi, (npart, col0) in enumerate(tmb):
                rms_rows(hh_[0:npart, bi, :], npart, D, stat, statb, bi, junk, junkb, [hb], eps6)
                ys, ysb = yst.nxt()
                cx.op("dve", lambda e: e.tensor_scalar(out=ys[0:npart, :], in0=hh_[0:npart, bi, :], scalar1=stat[0:npart, 16 + bi:17 + bi],
                                                       scalar2=None, op0=ALU.mult), r=[hb, statb], w=[ysb])
                cx.op("pool", lambda e: e.tensor_tensor(out=ys[0:npart, :], in0=ys[0:npart, :], in1=normfb[0:npart, :], op=ALU.mult),
                      r=[ysb, cb], w=[ysb])
                cx.dma("sp", out_dst(bi), ys[0:npart, :], r=[ysb], is_out=True)
            cx.barrier()
            sb.top = mk

        def inproj_q(N, rot_src, qdec_src, tm_units, T_):
            slabs = T_["slabs"]
            rot = sb.alloc([128, 4, N], F32, "rot")
            rotb_ = Buf()
            cx.dma("sp", rot[:, :, :], rot_src, w=[rotb_])
            qd = sb.alloc([128, RH * N], F32, "qd")
            cx.dma("sp", qd[:, :], qdec_src, w=[rotb_])
            f32r = T_["f32"]
            for which, (c0, c0s) in enumerate(((C_RQ, C_RQS), (C_RK, C_RKS))):
                sa, sab = slabs.nxt()
                load_slab(sa, sab, wi_s, c0, 512)
                ss_, ssb = slabs.nxt()
                load_slab(ss_, ssb, wi_s, c0s, 512)
                for h in range(RH):
                    ba, bb_ = nbank(), nbank()
                    fm_chunk(sa, sab, h, N, ba)
                    fm_chunk(ss_, ssb, h, N, bb_)
                    t1, t1b = f32r.nxt()
                    cx.op("dve", lambda e: e.tensor_tensor(out=t1[:, 0:N], in0=banks[ba][:, 0:N], in1=rot[:, 2 * which, :], op=ALU.mult),
                          r=[bankb[ba], rotb_], w=[t1b])
                    t2, t2b = f32r.nxt()
                    cx.op("dve", lambda e: e.tensor_tensor(out=t2[:, 0:N], in0=banks[bb_][:, 0:N], in1=rot[:, 2 * which + 1, :], op=ALU.mult),
                          r=[bankb[bb_], rotb_], w=[t2b])
                    if which == 0:
                        cx.op("pool", lambda e: e.tensor_tensor(out=t1[:, 0:N], in0=t1[:, 0:N], in1=t2[:, 0:N], op=ALU.add),
                              r=[t1b, t2b], w=[t1b])
                        cx.op("act", lambda e: e.copy(out=T_["Qr"][:, h, 0:N], in_=t1[:, 0:N]), r=[t1b], w=[T_["qkb"]])
                        cx.op("pool", lambda e: e.tensor_tensor(out=T_["Qrd"][:, h, 0:N], in0=t1[:, 0:N], in1=qd[:, h * N:(h + 1) * N], op=ALU.mult),
                              r=[t1b, rotb_], w=[T_["qkb"]])
                    else:
                        cx.op("pool", lambda e: e.tensor_tensor(out=T_["Kr"][:, h, 0:N], in0=t1[:, 0:N], in1=t2[:, 0:N], op=ALU.add),
                              r=[t1b, t2b], w=[T_["qkb"]])
            for half in range(2):
                sl, slb = slabs.nxt()
                load_slab(sl, slb, wi_s, C_RV + half * 512, 512)
                for (npart, col0, dst, dstb) in tm_units:
                    bk = nbank()
                    for kc in range(NKC):
                        cx.op("pe", lambda e: e.matmul(banks[bk][0:npart, :], lhsT=xnT[:, kc, col0:col0 + npart], rhs=sl[:, kc, :],
                                                       start=(kc == 0), stop=(kc == NKC - 1)), r=[slb, xnTb], w=[bankb[bk]])
                    cp("act" if half == 0 else "dve", dst[0:npart, half * 512:(half + 1) * 512], banks[bk][0:npart, :], [bankb[bk]], [dstb])
            for half in range(2):
                sl, slb = slabs.nxt()
                load_slab(sl, slb, wi_s, C_RG + half * 512, 512)
                for oc in range(4):
                    bk = nbank()
                    fm_chunk(sl, slb, oc, N, bk)
                    cx.op("act", lambda e: e.activation(out=T_["srg"][:, half * 4 + oc, 0:N], in_=banks[bk][:, 0:N], func=AF.Silu,
                                                        bias=zcol[:, :], scale=1.0), r=[bankb[bk], cb], w=[T_["srgb"]])
            for half in range(2):
                sl, slb = slabs.nxt()
                load_slab(sl, slb, wi_s, C_DQ + half * 512, 512)
                for oc in range(4):
                    bk = nbank()
                    fm_chunk(sl, slb, oc, N, bk)
                    cp("dve" if oc % 2 else "act", QT[:, half * 4 + oc, 0:N], banks[bk][:, 0:N], [bankb[bk]], [QTb])

        if 2 in phases:
            for j in range(NOWN):
                N = TS
                if j == 0:
                    cx.dma("sp", xtA[:, :, :], x_own[0, :, :].rearrange("(b p) d -> p b d", p=128), w=[xtAb])
                mk = sb.top
                xnA = [xn_tmp] + [sb.alloc([128, D], F32, "xnA") for _ in range(3)]
                xnAb = [xn_tmpb] + [Buf() for _ in range(3)]
                rms_scale(xtA, xtAb, 128, 4, xnA, xnAb, junk, junkb, stat, statb, soff=0)
                rms_transp(128, 4, xnT, xnTb, xnA, xnAb, (0, 1))
                cx.barrier()
                sb.top = mk
                T_ = {}
                T_["Qr"] = sb.alloc([128, RH, N], BF16, "Qr")
                T_["Qrd"] = sb.alloc([128, RH, N], BF16, "Qrd")
                T_["Kr"] = sb.alloc([128, RH, N], BF16, "Kr")
                T_["qkb"] = Buf()
                T_["srg"] = sb.alloc([128, 8, N], BF16, "srg")
                T_["srgb"] = Buf()
                rvb = sb.alloc([128, 4, RH * RDV], BF16, "rvb2")
                rvbb = Buf()
                mkip = sb.top
                T_["slabs"] = Ring([128, NKC, 512], BF16, 4, "slab")
                T_["f32"] = Ring([128, N], F32, 4, "f32")
                tm_units = [(128, blk * 128, rvb[:, blk, :], rvbb) for blk in range(4)]
                inproj_q(N, rot_fm[j, :, :, :].rearrange("f p n -> p f n"), bc_ap(qdec, 0, RH * TS), tm_units, T_)
                cx.barrier()
                sb.top = mkip
                dtf = sb.alloc([128, 4, N], F32, "dtf")
                dtfb = Buf()
                dtb_ = sb.alloc([128, 16, N], BF16, "dtb")
                dtbb = Buf()
                for q4 in range(4):
                    cx.dma("sp", dtf[:, :, :], dt_tab[:, q4 * 4:(q4 + 1) * 4, :], w=[dtfb])
                    cp("act" if q4 % 2 == 0 else "dve", dtb_[:, q4 * 4:(q4 + 1) * 4, :], dtf[:, :, :], [dtfb], [dtbb])
                snA = sb.alloc([128, RH, RDV], BF16, "snA")
                snB = sb.alloc([128, RH, RDV], BF16, "snB")
                sin_ = sb.alloc([128, RH, RDV], BF16, "sin")
                snb_ = Buf()
                sinb = Buf()
                cx.dma("sp", snA[:, :, :], snaps[2 * j, :, :, :].rearrange("h p e -> p h e"), w=[snb_])
                cx.dma("sp", snB[:, :, :], snaps[2 * j + 1, :, :, :].rearrange("h p e -> p h e"), w=[snb_])
                cx.op("dve", lambda e: e.tensor_scalar(out=sin_[:, :, :], in0=snA[:, :, :], scalar1=selcs[:, 0:1], scalar2=None, op0=ALU.mult),
                      r=[snb_, cb], w=[sinb])
                cx.op("dve", lambda e: e.scalar_tensor_tensor(out=sin_[:, :, :], in0=snB[:, :, :], scalar=selcs[:, 1:2], in1=sin_[:, :, :],
                                                              op0=ALU.mult, op1=ALU.add), r=[snb_, cb, sinb], w=[sinb])
                TR = {}
                ret_bufs(TR, N)

                def mk_ret(h):
                    kbs = [dict(npk=128, k0=kb * 128, q0=kb * 128, v=rvb[:, kb, h * RDV:(h + 1) * RDV], vb=rvbb,
                                dt=dtb_[:, h * 4 + kb, kb * 128:N], dtb=dtbb) for kb in range(4)]
                    return lambda bs: ret_unit(h, T_["Qr"][:, h, :], T_["Qrd"][:, h, :], T_["Kr"][:, h, :], T_["qkb"], N, kbs,
                                               sin_[:, h, :], sinb, [T_["srg"][:, h * 2 + d_, 0:N] for d_ in range(2)], T_["srgb"],
                                               [yretT[:, h * 2 + d_, 0:N] for d_ in range(2)], yretb, TR, bs=bs)

                run_interleaved([mk_ret(h) for h in range(RH)])
                cx.barrier()
                sb.top = markQ
                nkt = 2 * j + 2
                nblk = 4 * nkt
                nk = NMETA + TS * nkt
                T_ = {}
                T_["PT"] = Ring([128, 2, N], BF16, 3, "PT")
                for nm_ in ("o0s", "o1s", "m0", "a"):
                    T_[nm_] = sb.alloc([128, N], F32, nm_)
                T_["ob"] = Buf()
                T_["sq"] = Ring([128, N], BF16, 2, "sq")
                KTr = Ring([128, nk], BF16, 2, "KTh")
                Vr = Ring([128, nblk, DDV], BF16, 2, "Vh")
                Vmr = Ring([NMETA, DDV], BF16, 2, "Vmh")
                Twr = Ring([128, 9, TS], BF16, 2, "Tw")
                Tmr = Ring([NMETA, TS], BF16, 2, "Tm")
                for h in range(DH_):
                    kt, ktb_ = KTr.nxt()
                    cx.dma("sp", kt[:, :], KTs[h, :, 0:nk], w=[ktb_])
                    vt, vtb = Vr.nxt()
                    cx.dma("sp", vt[:, :, :], Vs[h, :, 0:nblk, :], w=[vtb])
                    vm, vmb = Vmr.nxt()
                    cx.dma("sp", vm[:, :], Vms[h, :, :], w=[vmb])
                    tw, twb = Twr.nxt()
                    cx.dma("sp", tw[:, :, :], Ts[h, :, :, :], w=[twb])
                    tm_, tmb_ = Tmr.nxt()
                    if j == 0:
                        cx.dma("sp", tm_[:, :], Tms[h, :, :], w=[tmb_])
                    blocks = [dict(npk=NMETA, kT=kt[:, 0:NMETA], kb=ktb_, v=vm[:, :], vb=vmb,
                                   T=(tm_[:, :] if j == 0 else None), Tb=tmb_, bias=(zcol if j == 0 else cbias[:, h:h + 1]))]
                    for kb in range(nblk):
                        p = kb - (nblk - 9)
                        inwin = p >= 0
                        blocks.append(dict(npk=128, kT=kt[:, NMETA + kb * 128:NMETA + (kb + 1) * 128], kb=ktb_, v=vt[:, kb, :], vb=vtb,
                                           T=(tw[:, p, :] if inwin else None), Tb=twb, bias=(zcol if inwin else cbias[:, h:h + 1])))
                    attn_unit(QT[:, h, :], QTb, N, blocks, ydiffT[:, h, 0:N], ydiffb, T_)
                run_pending()
                cx.barrier()
                sb.top = markQ
                def prefetch_next(j=j):
                    if j + 1 < NOWN:
                        cx.dma("sp", xtA[:, :, :], x_own[j + 1, :, :].rearrange("(b p) d -> p b d", p=128), w=[xtAb])

                tail(N, [(128, blk * 128) for blk in range(4)], None,
                     lambda bi: y_own[j, bi * 128:(bi + 1) * 128, :], xt_given=(xtA, xtAb), after_wo=prefetch_next)

        if 3 in phases:
            N = 32
            mk = sb.top
            xt = sb.alloc([128, 1, D], F32, "xt")
            xtb = Buf()
            cx.dma("sp", xt[0:32, 0, :], xs_in[:, :, :].rearrange("s t d -> (s t) d"), w=[xtb])
            rms_to_fm(xt, xtb, 32, 1, xnT, xnTb, xn_tmp, xn_tmpb, junk, junkb, stat, statb, (0, 1))
            cx.barrier()
            sb.top = mk
            T_ = {}
            T_["Qr"] = sb.alloc([128, RH, N], BF16, "Qr")
            T_["Qrd"] = sb.alloc([128, RH, N], BF16, "Qrd")
            T_["Kr"] = sb.alloc([128, RH, N], BF16, "Kr")
            T_["qkb"] = Buf()
            T_["srg"] = sb.alloc([128, 8, N], BF16, "srg")
            T_["srgb"] = Buf()
            rvs = [sb.alloc([16, RH * RDV], BF16, "rvs") for _ in range(2)]
            rvsb = [Buf(), Buf()]
            mkip = sb.top
            T_["slabs"] = Ring([128, NKC, 512], BF16, 4, "slab")
            T_["f32"] = Ring([128, N], F32, 4, "f32")
            tm_units = [(16, s_ * 16, rvs[s_], rvsb[s_]) for s_ in range(2)]
            inproj_q(N, rot_fm_s[:, :, :].rearrange("f p n -> p f n"), bc_ap(qdec_s, 0, RH * 32), tm_units, T_)
            cx.barrier()
            sb.top = mkip
            dsf = sb.alloc([16, RH, 16], F32, "dsf")
            dsb = sb.alloc([16, RH, 16], BF16, "dsb")
            dsbb = Buf()
            cx.dma("sp", dsf[:, :, :], dt_s[:, :, :], w=[dsbb])
            cx.op("dve", lambda e: e.tensor_copy(out=dsb[:, :, :], in_=dsf[:, :, :]), r=[dsbb], w=[dsbb])
            sin_ = sb.alloc([128, RH, RDV], BF16, "sin")
            sinb = Buf()
            TR = {}
            ret_bufs(TR, 16)
            sins = [sb.alloc([128, RH, RDV], BF16, "sin") for _ in range(2)]
            sinbs = [Buf(), Buf()]
            makers = []
            for s_ in range(2):
                cx.dma("sp", sins[s_][:, :, :], snap_s[s_, :, :, :].rearrange("h p e -> p h e"), w=[sinbs[s_]])

                def mk_ret_s(h, s_=s_):
                    c0 = s_ * 16
                    kbs = [dict(npk=16, k0=0, q0=0, v=rvs[s_][:, h * RDV:(h + 1) * RDV], vb=rvsb[s_], dt=dsb[:, h, :], dtb=dsbb)]
                    return lambda bs: ret_unit(h, T_["Qr"][:, h, c0:c0 + 16], T_["Qrd"][:, h, c0:c0 + 16], T_["Kr"][:, h, c0:c0 + 16],
                                               T_["qkb"], 16, kbs, sins[s_][:, h, :], sinbs[s_],
                                               [T_["srg"][:, h * 2 + d_, c0:c0 + 16] for d_ in range(2)], T_["srgb"],
                                               [yretT[:, h * 2 + d_, c0:c0 + 16] for d_ in range(2)], yretb, TR, bs=bs)

                makers += [mk_ret_s(h) for h in range(RH)]
            run_interleaved(makers)
            cx.barrier()
            sb.top = markQ
            T_ = {}
            T_["PT"] = Ring([128, 2, 16], BF16, 3, "PT")
            for nm_ in ("o0s", "o1s", "m0", "a"):
                T_[nm_] = sb.alloc([128, 16], F32, nm_)
            T_["ob"] = Buf()
            T_["sq"] = Ring([128, 16], BF16, 2, "sq")
            KcT = sb.alloc([128, DH_, P], BF16, "KcT")
            KcTb = Buf()
            Vc = sb.alloc([128, PB, D], BF16, "Vc")
            Vcb = Buf()
            ckr = Ring([128, D], F32, 2, "ckr")
            KnT = sb.alloc([128, DH_, 16], BF16, "KnT")
            Vnn = sb.alloc([16, DH_, DDV], BF16, "Vnn")
            Ts1 = sb.alloc([128, DH_, 16], BF16, "Ts1")
            Ts2 = sb.alloc([16, DH_, 16], BF16, "Ts2")
            nwb = Buf()
            cx.dma("sp", Ts1[:, :, :], Tss[:, :, :].rearrange("h p q -> p h q"), w=[nwb])
            cx.dma("sp", Ts2[:, :, :], Tsn[:, :, :].rearrange("h p q -> p h q"), w=[nwb])
            for s_ in range(2):
                cx.dma("sp", KnT[:, :, :], KTn[s_, :, :, :].rearrange("h p q -> p h q"), w=[nwb])
                cx.dma("sp", Vnn[:, :, :], Vn[s_, :, :, :].rearrange("h p e -> p h e"), w=[nwb])
                for blk in range(PB):
                    ck, ckb = ckr.nxt()
                    cx.dma("sp", ck[:, :], ck_in[s_, blk * 128:(blk + 1) * 128, :], w=[ckb])
                    for half in range(2):
                        bk = nbank()
                        for hh in range(4):
                            h = half * 4 + hh
                            cx.op("pe", lambda e: e.transpose(out=banks[bk][:, hh * 128:(hh + 1) * 128], in_=ck[:, h * 128:(h + 1) * 128],
                                                              identity=ident[:, :]), r=[ckb, cb], w=[bankb[bk]])
                        cp("act" if half == 0 else "dve", KcT[:, half * 4:half * 4 + 4, blk * 128:(blk + 1) * 128],
                           banks[bk][:, :].rearrange("p (h t) -> p h t", h=4), [bankb[bk]], [KcTb])
                    cv, cvb = ckr.nxt()
                    cx.dma("sp", cv[:, :], cv_in[s_, blk * 128:(blk + 1) * 128, :], w=[cvb])
                    cx.op("pool", lambda e: e.tensor_copy(out=Vc[:, blk, :], in_=cv[:, :]), r=[cvb], w=[Vcb])
                c0 = s_ * 16
                for h in range(DH_):
                    blocks = []
                    for kb in range(PB):
                        last = kb == PB - 1
                        blocks.append(dict(npk=128, kT=KcT[:, h, kb * 128:(kb + 1) * 128], kb=KcTb, v=Vc[:, kb, h * DDV:(h + 1) * DDV], vb=Vcb,
                                           T=(Ts1[:, h, :] if last else None), Tb=nwb, bias=(zcol if last else cbias[:, h:h + 1])))
                    blocks.append(dict(npk=16, kT=KnT[:, h, :], kb=nwb, v=Vnn[:, h, :], vb=nwb, T=Ts2[:, h, :], Tb=nwb, bias=zcol))
                    attn_unit(QT[:, h, c0:c0 + 16], QTb, 16, blocks, ydiffT[:, h, c0:c0 + 16], ydiffb, T_)
            run_pending()
            cx.barrier()
            sb.top = markQ
            tail(32, [(32, 0)],
                 lambda xt_, xtb_: cx.dma("sp", xt_[0:32, 0, :], xs_in[:, :, :].rearrange("s t d -> (s t) d"), w=[xtb_]),
                 lambda bi: y_s[:, :])

    cx.finish()
    return nc, cx


def _rot_fm_tables(pos):
    half = 64
    freq = (1.0 / (10000.0 ** np.linspace(0.0, 1.0, half, dtype=np.float32))).astype(np.float32)
    ang = (np.asarray(pos, np.float32)[None, :] * freq[:, None]).astype(np.float32)
    c = np.concatenate([np.cos(ang), np.cos(ang)], axis=0)
    s = np.concatenate([-np.sin(ang), np.sin(ang)], axis=0)
    sc = np.float32(RDK ** -0.5)
    return np.stack([c * sc, s * sc, c, s]).astype(np.float32)


def _host_consts(SEQ, P):
    LTOK = NMETA + SEQ
    half = 64
    freq = (1.0 / (10000.0 ** np.linspace(0.0, 1.0, half, dtype=np.float32))).astype(np.float32)
    pos = np.concatenate([np.arange(-NMETA, 0), np.arange(SEQ)]).astype(np.float32)
    ang = (pos[:, None] * freq[None, :]).astype(np.float32)
    rot_tm = np.concatenate([np.cos(ang), np.sin(ang)], axis=1).astype(np.float32)
    pos_s = np.arange(P, P + 16).astype(np.float32)
    ang_s = (pos_s[:, None] * freq[None, :]).astype(np.float32)
    rot_tm_s = np.concatenate([np.cos(ang_s), np.sin(ang_s)], axis=1).astype(np.float32)
    rfs = _rot_fm_tables(pos_s)
    rot_fm_s = np.ascontiguousarray(np.concatenate([rfs, rfs], axis=2))
    i = np.arange(TS, dtype=np.float64)
    kdec_t = np.stack([np.exp(LG[h] * (TS - 1.0 - i)) for h in range(RH)], axis=1).astype(np.float32)
    im = np.arange(NMETA, dtype=np.float64)
    kdec_m = np.stack([np.exp(LG[h] * (NMETA - 1.0 - im)) for h in range(RH)], axis=1).astype(np.float32)
    qdec = np.concatenate([np.exp(LG[h] * (i + 1.0)) for h in range(RH)]).astype(np.float32)[None]
    qd16 = [np.exp(LG[h] * (im + 1.0)) for h in range(RH)]
    qdec_s = np.concatenate([np.concatenate([q, q]) for q in qd16]).astype(np.float32)[None]
    m = (np.arange(4)[None, :, None] * 128 + np.arange(128)[:, None, None]).astype(np.int64)
    n = np.arange(TS)[None, None, :].astype(np.int64)
    dt = np.zeros((128, 16, TS), np.float32)
    for h in range(RH):
        dcau = np.where(m <= n, np.exp(LG[h] * (n - m).astype(np.float64)), 0.0)
        dsame = np.where((m > n) & (m // CHUNK == n // CHUNK), np.exp(LG[h] * (m - n).astype(np.float64)), 0.0)
        dt[:, h * 4:(h + 1) * 4, :] = (dcau + dsame).astype(np.float32)
    a16 = np.abs(np.arange(16)[:, None] - np.arange(16)[None, :]).astype(np.float64)
    dt_s = np.stack([np.exp(LG[h] * a16) for h in range(RH)], axis=1).astype(np.float32)
    ident = np.eye(128, dtype=np.float32)
    jm = np.ascontiguousarray(ident[::-1])
    j16 = np.ascontiguousarray(np.eye(16, dtype=np.float32)[::-1])
    bs = t5_bucket_np(159 - np.arange(320))
    ohs = (bs[None, :] == np.arange(32)[:, None]).astype(np.float32)
    return dict(rot_tm=rot_tm, rot_tm_s=rot_tm_s, rot_fm_s=rot_fm_s, kdec_t=kdec_t, kdec_m=kdec_m, qdec=qdec, qdec_s=qdec_s,
                dt_tab=dt, dt_s=dt_s, c_ident=ident, c_j=jm, c_j16=j16, ohs=ohs)


def _core_consts(g, NOWN):
    rot = np.stack([_rot_fm_tables(np.arange((2 * j + g) * TS, (2 * j + g + 1) * TS)) for j in range(NOWN)])
    selc = np.zeros((128, 2), np.float32)
    selc[:, g] = 1.0
    bo = t5_bucket_np(1023 - 512 * g - np.arange(1664))
    ohr = (bo[None, :] == np.arange(32)[:, None]).astype(np.float32)
    k = np.arange(128)[:, None, None]
    p = np.arange(9)[None, :, None]
    q = np.arange(TS)[None, None, :]
    kr = (p - 1) * 128 - 512 * g + k
    vis = (kr // CHUNK) <= (q // CHUNK)
    mask8 = np.where(vis, 0.0, 8.0 * NEG).astype(np.float32)
    return dict(rot_fm=np.ascontiguousarray(rot), selc=selc, ohr=ohr, mask8=np.ascontiguousarray(mask8))


def _swap_halves_cols(w, c0, nheads, dk):
    cols = []
    for h in range(nheads):
        b = c0 + h * dk
        cols += list(range(b + dk // 2, b + dk)) + list(range(b, b + dk // 2))
    return w[:, cols]


def kernel(x_prompt, x_sample, cache_k, cache_v, state_ret, meta_tokens, rel_bias, norm1_g, w_in,
           lambda_q1, lambda_k1, lambda_q2, lambda_k2, diff_subln_g, w_ret_branch, w_diff_branch,
           w_o, norm2_g, w_ffn_up, w_ffn_down, normf_g, _phases=(1, 2, 3), _ret_raw=False):
    f = lambda a: np.ascontiguousarray(np.asarray(a, dtype=np.float32))
    x_prompt, x_sample, cache_k, cache_v, state_ret = map(f, (x_prompt, x_sample, cache_k, cache_v, state_ret))
    B, SEQ, _ = x_prompt.shape
    DB = x_sample.shape[0]
    P = cache_k.shape[2]
    NT = SEQ // TS
    NOWN = NT // 2
    ncores = 2 * B
    assert DB == 2 * ncores
    LTOK = NMETA + SEQ
    w_in0 = f(w_in)[0]
    w_inx = np.ascontiguousarray(np.concatenate(
        [w_in0, _swap_halves_cols(w_in0, C_RQ, RH, RDK), _swap_halves_cols(w_in0, C_RK, RH, RDK)], axis=1))
    hc = _host_consts(SEQ, P)
    common = dict(
        meta=f(meta_tokens), w_in=w_inx, w_rb=f(w_ret_branch)[0], w_db=f(w_diff_branch)[0], w_o=f(w_o)[0],
        w_up=f(w_ffn_up)[0], w_dn=f(w_ffn_down)[0],
        g1row=f(norm1_g)[0].reshape(1, D), g2row=f(norm2_g)[0].reshape(1, D),
        subg=f(diff_subln_g)[0].reshape(128, 1), normf=f(normf_g).reshape(1, D),
        lamv=np.concatenate([f(lambda_q1)[0], f(lambda_k1)[0], f(lambda_q2)[0], f(lambda_k2)[0]]).reshape(1, 4 * DDH),
        relb=f(rel_bias), **hc)
    nc, cx = build(SEQ, P, phases=_phases)
    cc = [_core_consts(g, NOWN) for g in range(2)]
    in_maps = []
    for c in range(ncores):
        b, g = c // 2, c % 2
        xa = x_prompt[b]
        xo = np.ascontiguousarray(xa.reshape(NT, TS, D)[g::2])
        m = dict(common)
        m.update(cc[g])
        m.update(x_all=xa, x_own=xo, xs_in=x_sample[2 * c:2 * c + 2],
                 ck_in=np.ascontiguousarray(cache_k[0, 2 * c:2 * c + 2].reshape(2, P, D)),
                 cv_in=np.ascontiguousarray(cache_v[0, 2 * c:2 * c + 2].reshape(2, P, D)),
                 st_in=np.ascontiguousarray(state_ret[0, 2 * c:2 * c + 2]))
        in_maps.append(m)
    res = run_bass_kernel_spmd(nc, in_maps, core_ids=list(range(ncores)))
    R = res.results
    if _ret_raw:
        return R
    k_rows_p = np.stack([R[2 * b]["k_rows"].reshape(LTOK, DH_, 2 * DDH) for b in range(B)])[None]
    v_rows_p = np.stack([R[2 * b]["v_rows"].reshape(LTOK, DH_, DDV) for b in range(B)])[None]
    ret_p = np.stack([R[2 * b]["ret_state"] for b in range(B)])[None]
    y_prompt = np.zeros((B, NT, TS, D), np.float32)
    for c in range(ncores):
        y_prompt[c // 2, (c % 2)::2] = R[c]["y_own"]
    y_prompt = y_prompt.reshape(B, SEQ, D)
    y_sample = np.concatenate([R[c]["y_s"].reshape(2, 16, D) for c in range(ncores)], axis=0)
    ks = np.concatenate([R[c]["k_rows_s"].reshape(2, 16, DH_, 2 * DDH) for c in range(ncores)], axis=0)[None]
    vs = np.concatenate([R[c]["v_rows_s"].reshape(2, 16, DH_, DDV) for c in range(ncores)], axis=0)[None]
    rs = np.concatenate([R[c]["ret_state_s"] for c in range(ncores)], axis=0)[None]
    return (y_prompt, y_sample, k_rows_p, v_rows_p, ret_p, ks, vs, rs)
```

```python
import math
import numpy as np
import concourse.bass as bass
import concourse.mybir as mybir
from concourse.bass_utils import run_bass_kernel_spmd

F32 = mybir.dt.float32
BF16 = mybir.dt.bfloat16
AF = mybir.ActivationFunctionType
ALU = mybir.AluOpType

D = 1024
NMETA = 16
CHUNK = 64
RH, RDK, RDV = 4, 128, 256
DH_, DDH, DDV = 8, 64, 128
DFF = 2816
NKC = 8
TS = 512
NEG = -30000.0
LAM_INIT = 0.8 - 0.6 * math.exp(-0.3 * 0)
GAM = [1.0 - 2.0 ** (-5.0 - h) for h in range(RH)]
LG = [math.log(x) for x in GAM]
C_RQ, C_RK, C_RV, C_RG, C_DQ, C_DK, C_DV, C_GR, C_GD = 0, 512, 1024, 2048, 3072, 4096, 5120, 6144, 7168
C_RQS, C_RKS = 8192, 8704
WINX = 9216


class Buf:
    __slots__ = ("w", "rs", "excl")

    def __init__(self, excl=False):
        self.w = None
        self.rs = []
        self.excl = excl


class Ctx:
    NDS = 24

    def __init__(self, nc):
        self.nc = nc
        self.eng = {"pe": nc.tensor, "act": nc.scalar, "dve": nc.vector, "pool": nc.gpsimd, "sp": nc.sync}
        self.sem = {k: nc.alloc_semaphore("c_" + k) for k in ("pe", "act", "dve", "pool")}
        self.cnt = {k: 0 for k in self.sem}
        self.waited = {k: {} for k in self.eng}
        self.dsem = [nc.alloc_semaphore("d%d" % i) for i in range(self.NDS)]
        self.dcnt = [0] * self.NDS
        self.dnext = 0
        self.out_toks = []
        self.n_ins = 0

    def _wait(self, e, tok):
        if tok is None:
            return
        key, val, sem = tok
        if self.waited[e].get(key, 0) >= val:
            return
        self.waited[e][key] = val
        self.eng[e].wait_ge(sem, val)

    def _deps(self, e, r, w, pe_skip=False):
        for b in r:
            if b.w is not None and not (pe_skip and b.w[0] == "pe"):
                self._wait(e, b.w)
            if b.excl:
                for t in b.rs:
                    if t[0] != e:
                        self._wait(e, t)
        for b in w:
            if b.w is not None and not (pe_skip and b.w[0] == "pe"):
                self._wait(e, b.w)
            for t in b.rs:
                if not (pe_skip and t[0] == "pe"):
                    self._wait(e, t)

    def _mark(self, tok, r, w):
        for b in r:
            b.rs = [t for t in b.rs if t[0] != tok[0]] + [tok]
        for b in w:
            b.w = tok
            b.rs = []

    def op(self, e, fn, r=(), w=()):
        self._deps(e, r, w, pe_skip=(e == "pe"))
        ins = fn(self.eng[e])
        self.cnt[e] += 1
        ins.then_inc(self.sem[e], 1)
        tok = (e, self.cnt[e], self.sem[e])
        self._mark(tok, r, w)
        self.n_ins += 1
        return tok

    def dma(self, q, out, in_, r=(), w=(), is_out=False):
        s = self.dnext
        self.dnext = (self.dnext + 1) % self.NDS
        self._deps(q, r, w)
        if self.dcnt[s] > 0:
            self._wait(q, ("d%d" % s, 16 * self.dcnt[s], self.dsem[s]))
        self.dcnt[s] += 1
        self.eng[q].dma_start(out=out, in_=in_).then_inc(self.dsem[s], 16)
        tok = ("d%d" % s, 16 * self.dcnt[s], self.dsem[s])
        self._mark(tok, r, w)
        if is_out:
            self.out_toks.append(tok)
        self.n_ins += 1
        return tok

    def barrier(self):
        toks = [(k, self.cnt[k], self.sem[k]) for k in self.sem if self.cnt[k] > 0]
        toks += [("d%d" % s, 16 * self.dcnt[s], self.dsem[s]) for s in range(self.NDS) if self.dcnt[s] > 0]
        for e in self.eng:
            for t in toks:
                self._wait(e, t)

    def finish(self):
        for s in range(self.NDS):
            if self.dcnt[s] > 0:
                self._wait("sp", ("d%d" % s, 16 * self.dcnt[s], self.dsem[s]))
        for k in self.sem:
            if self.cnt[k] > 0:
                self._wait("sp", (k, self.cnt[k], self.sem[k]))


class Sb:
    def __init__(self, nc, cap):
        self.base = 16512
        self.nc, self.top, self.cap, self.n = nc, 0, cap, 0

    def alloc(self, shape, dt, name="t"):
        esz = 4 if dt == F32 else 2
        per = 1
        for s in shape[1:]:
            per *= s
        nbytes = (per * esz + 63) // 64 * 64
        assert self.top + nbytes <= self.cap, ("SBUF overflow", name, self.top, nbytes)
        self.n += 1
        t = self.nc.alloc_sbuf_tensor_at("%s_%d" % (name, self.n), list(shape), dt, offset=self.base + self.top)
        self.top += nbytes
        return t


def t5_bucket_np(rel):
    import jax
    import jax.numpy as jnp
    with jax.default_device(jax.devices("cpu")[0]):
        rel = jnp.asarray(np.asarray(rel, dtype=np.int32))
        nb, max_exact = 16, 8
        ret = jnp.where(rel > 0, nb, 0)
        n = jnp.abs(rel)
        nf = jnp.maximum(n, max_exact).astype(jnp.float32)
        large = max_exact + (jnp.log(nf / max_exact) / math.log(128 / max_exact) * (nb - max_exact)).astype(jnp.int32)
        large = jnp.minimum(large, nb - 1)
        return np.asarray(ret + jnp.where(n < max_exact, n, large))


def build(SEQ, P, phases=(1, 2, 3)):
    NT = SEQ // TS
    NOWN = NT // 2
    LTOK = NMETA + SEQ
    PB = P // 128
    nc = bass.Bass("TRN2", target_bir_lowering=False)
    cx = Ctx(nc)

    def din(name, shape, dt=F32):
        return nc.dram_tensor(name, list(shape), dt, kind="ExternalInput")

    def dout(name, shape, dt=F32):
        return nc.dram_tensor(name, list(shape), dt, kind="ExternalOutput")

    def dscr(name, shape, dt):
        return nc.dram_tensor(name, list(shape), dt)

    def bc_ap(t, off, n):
        return bass.AP(tensor=t, offset=off, ap=[[0, 128], [1, n]])

    x_all = din("x_all", [SEQ, D])
    x_own = din("x_own", [NOWN, TS, D])
    meta = din("meta", [NMETA, D])
    w_in = din("w_in", [D, WINX])
    w_rb = din("w_rb", [D, D])
    w_db = din("w_db", [D, D])
    w_o = din("w_o", [D, D])
    w_up = din("w_up", [D, 2 * DFF])
    w_dn = din("w_dn", [DFF, D])
    g1row = din("g1row", [1, D])
    g2row = din("g2row", [1, D])
    subg = din("subg", [128, 1])
    normf = din("normf", [1, D])
    lamv = din("lamv", [1, 4 * DDH])
    relb = din("relb", [32, DH_])
    c_ident = din("c_ident", [128, 128])
    c_j = din("c_j", [128, 128])
    c_j16 = din("c_j16", [16, 16])
    rot_tm = din("rot_tm", [LTOK, 128])
    rot_tm_s = din("rot_tm_s", [16, 128])
    kdec_t = din("kdec_t", [TS, RH])
    kdec_m = din("kdec_m", [NMETA, RH])
    rot_fm = din("rot_fm", [NOWN, 4, 128, TS])
    rot_fm_s = din("rot_fm_s", [4, 128, 32])
    dt_tab = din("dt_tab", [128, 16, TS])
    dt_s = din("dt_s", [16, RH, 16])
    qdec = din("qdec", [1, RH * TS])
    qdec_s = din("qdec_s", [1, RH * 32])
    selc = din("selc", [128, 2])
    ohr = din("ohr", [32, 1664])
    ohs = din("ohs", [32, 320])
    mask8 = din("mask8", [128, 9, TS])
    xs_in = din("xs_in", [2, 16, D])
    ck_in = din("ck_in", [2, P, D])
    cv_in = din("cv_in", [2, P, D])
    st_in = din("st_in", [2, RH, RDK, RDV])

    k_rows = dout("k_rows", [LTOK, D])
    v_rows = dout("v_rows", [LTOK, D])
    ret_state = dout("ret_state", [RH, RDK, RDV])
    y_own = dout("y_own", [NOWN, TS, D])
    y_s = dout("y_s", [32, D])
    k_rows_s = dout("k_rows_s", [32, D])
    v_rows_s = dout("v_rows_s", [32, D])
    ret_state_s = dout("ret_state_s", [2, RH, RDK, RDV])

    wi_s = dscr("wi_s", [D, WINX], BF16)
    wrb_s = dscr("wrb_s", [D, D], BF16)
    wdb_s = dscr("wdb_s", [D, D], BF16)
    wo_s = dscr("wo_s", [D, D], BF16)
    wup_s = dscr("wup_s", [D, 2 * DFF], BF16)
    wdn_s = dscr("wdn_s", [DFF, D], BF16)
    KTs = dscr("KTs", [DH_, 128, LTOK], BF16)
    Vs = dscr("Vs", [DH_, 128, SEQ // 128, DDV], BF16)
    Vms = dscr("Vms", [DH_, NMETA, DDV], BF16)
    KTn = dscr("KTn", [2, DH_, 128, 16], BF16)
    Vn = dscr("Vn", [2, DH_, 16, DDV], BF16)
    snaps = dscr("snaps", [NT + 1, RH, 128, RDV], BF16)
    snap_s = dscr("snap_s", [2, RH, 128, RDV], BF16)
    FRs = dscr("FRs", [DH_, 1664], BF16)
    FRss = dscr("FRss", [DH_, 320], BF16)
    Ts = dscr("Ts", [DH_, 128, 9, TS], BF16)
    Tms = dscr("Tms", [DH_, 16, TS], BF16)
    Tss = dscr("Tss", [DH_, 128, 16], BF16)
    Tsn = dscr("Tsn", [DH_, 16, 16], BF16)

    sb = Sb(nc, 207 * 1024)
    pairs = [nc.alloc_psum_tensor("pair%d" % i, [128, 1024], F32) for i in range(4)]
    banks = [pairs[i // 2][:, (i % 2) * 512:(i % 2 + 1) * 512] for i in range(8)]
    bankb = [Buf(excl=True) for _ in range(8)]

    ident = sb.alloc([128, 128], F32, "ident")
    identb = sb.alloc([128, 128], BF16, "identb")
    jmatb = sb.alloc([128, 128], BF16, "jmatb")
    j16b = sb.alloc([16, 16], BF16, "j16b")
    onesb = sb.alloc([128, 128], BF16, "onesb")
    subgs = sb.alloc([128, 1], F32, "subgs")
    eps6 = sb.alloc([128, 1], F32, "eps6")
    eps5 = sb.alloc([128, 1], F32, "eps5")
    zcol = sb.alloc([128, 1], F32, "zcol")
    cbias = sb.alloc([128, DH_], F32, "cbias")
    neglam = sb.alloc([128, 1], F32, "neglam")
    selcs = sb.alloc([128, 2], F32, "selcs")
    normfb = sb.alloc([128, D], F32, "normfb")
    cb = Buf()
    ctmp = sb.alloc([128, 128], F32, "ctmp")
    cx.dma("sp", ident[:, :], c_ident[:, :], w=[cb])
    cx.dma("sp", ctmp[:, :], c_j[:, :], w=[cb])
    cx.dma("sp", subgs[:, :], subg[:, :], w=[cb])
    cx.dma("sp", selcs[:, :], selc[:, :], w=[cb])
    cx.dma("sp", cbias[:, :], bc_ap(relb, 15 * DH_, DH_), w=[cb])
    cx.dma("sp", normfb[:, :], bc_ap(normf, 0, D), w=[cb])
    cx.op("dve", lambda e: e.memset(onesb[:, :], 1.0), w=[cb])
    cx.op("dve", lambda e: e.memset(eps6[:, :], 1e-6), w=[cb])
    cx.op("dve", lambda e: e.memset(eps5[:, :], 1e-5), w=[cb])
    cx.op("dve", lambda e: e.memset(zcol[:, :], 0.0), w=[cb])
    cx.op("dve", lambda e: e.tensor_copy(out=identb[:, :], in_=ident[:, :]), r=[cb], w=[cb])
    cx.op("dve", lambda e: e.tensor_copy(out=jmatb[:, :], in_=ctmp[:, :]), r=[cb], w=[cb])
    cx.op("dve", lambda e: e.tensor_scalar(out=subgs[:, :], in0=subgs[:, :], scalar1=float(1.0 - LAM_INIT), scalar2=None, op0=ALU.mult),
          r=[cb], w=[cb])
    cx.dma("sp", ctmp[0:16, 0:16], c_j16[:, :], r=[cb], w=[cb])
    cx.op("dve", lambda e: e.tensor_copy(out=j16b[:, :], in_=ctmp[0:16, 0:16]), r=[cb], w=[cb])
    lvt = sb.alloc([128, 4 * DDH], F32, "lvt")
    lpr = sb.alloc([128, 2 * DDH], F32, "lpr")
    lsc = sb.alloc([128, 4], F32, "lsc")
    cx.dma("sp", lvt[:, :], bc_ap(lamv, 0, 4 * DDH), w=[cb])
    cx.op("dve", lambda e: e.tensor_tensor(out=lpr[:, 0:64], in0=lvt[:, 0:64], in1=lvt[:, 64:128], op=ALU.mult), r=[cb], w=[cb])
    cx.op("dve", lambda e: e.tensor_tensor(out=lpr[:, 64:128], in0=lvt[:, 128:192], in1=lvt[:, 192:256], op=ALU.mult), r=[cb], w=[cb])
    cx.op("dve", lambda e: e.reduce_sum(out=lsc[:, 0:1], in_=lpr[:, 0:64], axis=mybir.AxisListType.X), r=[cb], w=[cb])
    cx.op("dve", lambda e: e.reduce_sum(out=lsc[:, 1:2], in_=lpr[:, 64:128], axis=mybir.AxisListType.X), r=[cb], w=[cb])
    cx.op("act", lambda e: e.activation(out=lsc[:, 2:4], in_=lsc[:, 0:2], func=AF.Exp, bias=zcol[:, :], scale=1.0), r=[cb], w=[cb])
    cx.op("dve", lambda e: e.tensor_tensor(out=neglam[:, :], in0=lsc[:, 3:4], in1=lsc[:, 2:3], op=ALU.subtract), r=[cb], w=[cb])
    cx.op("dve", lambda e: e.tensor_scalar(out=neglam[:, :], in0=neglam[:, :], scalar1=float(-LAM_INIT), scalar2=None, op0=ALU.add),
          r=[cb], w=[cb])

    mark0 = sb.top
    g1b = sb.alloc([128, D], F32, "g1b")
    g2b = sb.alloc([128, D], F32, "g2b")
    cx.dma("sp", g1b[:, :], bc_ap(g1row, 0, D), w=[cb])
    cx.dma("sp", g2b[:, :], bc_ap(g2row, 0, D), w=[cb])
    mark0 = sb.top
    n_once = [0]

    def cast_dma(dst_ap, src_ap):
        sem = nc.alloc_semaphore("w%d" % n_once[0])
        n_once[0] += 1
        nc.gpsimd.dma_start(out=dst_ap, in_=src_ap).then_inc(sem, 16)
        b = Buf()
        b.w = ("w%d" % n_once[0], 16, sem)
        return b

    wb = {}
    wb["in_kv"] = [cast_dma(wi_s[:, 512:2048], w_in[:, 512:2048]), cast_dma(wi_s[:, 4096:6144], w_in[:, 4096:6144])]
    wb["in_q"] = [cast_dma(wi_s[:, 0:512], w_in[:, 0:512]), cast_dma(wi_s[:, 2048:4096], w_in[:, 2048:4096]),
                  cast_dma(wi_s[:, 6144:WINX], w_in[:, 6144:WINX])]
    if 2 in phases or 3 in phases:
        wb["rb"] = [cast_dma(wrb_s[:, :], w_rb[:, :])]
        wb["db"] = [cast_dma(wdb_s[:, :], w_db[:, :])]
        wb["o"] = [cast_dma(wo_s[:, :], w_o[:, :])]
        wb["up"] = [cast_dma(wup_s[h_ * 512:(h_ + 1) * 512, :], w_up[h_ * 512:(h_ + 1) * 512, :]) for h_ in range(2)]
        wb["dn"] = [cast_dma(wdn_s[h_ * 1408:(h_ + 1) * 1408, :], w_dn[h_ * 1408:(h_ + 1) * 1408, :]) for h_ in range(2)]
    wkey = {}
    for nm_, t_ in (("in", wi_s), ("rb", wrb_s), ("db", wdb_s), ("o", wo_s), ("up", wup_s), ("dn", wdn_s)):
        wkey[t_.name] = nm_
    cast_iter = iter(())

    def load_slab(dst, dstb, wsrc, c0, ncols, nkc=NKC, k0=0, q="sp"):
        src = wsrc[k0 * 128:(k0 + nkc) * 128, c0:c0 + ncols].rearrange("(kc p) n -> p kc n", p=128)
        nm_ = wkey[wsrc.name]
        if nm_ == "in":
            deps = wb["in_kv"] if (512 <= c0 < 2048 or 4096 <= c0 < 6144) else wb["in_q"]
        else:
            deps = wb[nm_]
        return cx.dma(q, dst[:, 0:nkc, 0:ncols], src, r=deps, w=[dstb])

    def cp(eng, out, in_, r, w):
        if eng == "act":
            return cx.op("act", lambda e: e.copy(out=out, in_=in_), r=r, w=w)
        return cx.op(eng, lambda e: e.tensor_copy(out=out, in_=in_), r=r, w=w)

    def rms_rows(xt_ap, npart, dim, stat, statb, col, junk, junkb, rb, eps_ap):
        cx.op("act", lambda e: e.activation(out=junk[0:npart, 0:dim], in_=xt_ap, func=AF.Square,
                                            accum_out=stat[0:npart, col:col + 1]), r=rb, w=[junkb, statb])
        cx.op("act", lambda e: e.activation(out=stat[0:npart, 8 + col:9 + col], in_=stat[0:npart, col:col + 1], func=AF.Sqrt,
                                            bias=eps_ap[0:npart, :], scale=1.0 / dim), r=[statb, cb], w=[statb])
        cx.op("dve", lambda e: e.reciprocal(out=stat[0:npart, 16 + col:17 + col], in_=stat[0:npart, 8 + col:9 + col]),
              r=[statb], w=[statb])

    def rms_scale(xt, xtb, npart, nblk, xn_list, xnb_list, junk, junkb, stat, statb, gainb=None, soff=0):
        gb_ = g1b if gainb is None else gainb
        for blk in range(nblk):
            xn_tmp, xn_tmpb = xn_list[blk % len(xn_list)], xnb_list[blk % len(xn_list)]
            rms_rows(xt[0:npart, blk, :], npart, D, stat, statb, soff + blk, junk, junkb, [xtb], eps6)
            cx.op("dve", lambda e: e.scalar_tensor_tensor(out=xn_tmp[0:npart, :], in0=xt[0:npart, blk, :],
                                                          scalar=stat[0:npart, 16 + soff + blk:17 + soff + blk], in1=gb_[0:npart, :],
                                                          op0=ALU.mult, op1=ALU.mult), r=[xtb, statb, cb], w=[xn_tmpb])

    def rms_transp(npart, nblk, xnT, xnTb, xn_list, xnb_list, pb):
        for blk in range(nblk):
            xn_tmp, xn_tmpb = xn_list[blk % len(xn_list)], xnb_list[blk % len(xn_list)]
            for half in range(2):
                bk = pb[half]
                for kk in range(4):
                    kc = half * 4 + kk
                    cx.op("pe", lambda e: e.transpose(out=banks[bk][:, kk * 128:kk * 128 + npart],
                                                      in_=xn_tmp[0:npart, kc * 128:(kc + 1) * 128],
                                                      identity=ident[0:npart, 0:npart]),
                          r=[xn_tmpb, cb], w=[bankb[bk]])
                src = banks[bk][:, :].rearrange("p (k t) -> p k t", k=4)[:, :, 0:npart]
                dst = xnT[:, half * 4:half * 4 + 4, blk * npart:(blk + 1) * npart]
                cp("act" if half == 0 else "dve", dst, src, [bankb[bk]], [xnTb])

    def rms_to_fm(xt, xtb, npart, nblk, xnT, xnTb, xn_tmp, xn_tmpb, junk, junkb, stat, statb, pb, gainb=None):
        xn_list, xnb_list = (xn_tmp, xn_tmpb) if isinstance(xn_tmp, list) else ([xn_tmp], [xn_tmpb])
        for blk in range(nblk):
            xl, xbl = [xn_list[blk % len(xn_list)]], [xnb_list[blk % len(xn_list)]]
            gb_ = g1b if gainb is None else gainb
            rms_rows(xt[0:npart, blk, :], npart, D, stat, statb, blk, junk, junkb, [xtb], eps6)
            cx.op("dve", lambda e: e.scalar_tensor_tensor(out=xl[0][0:npart, :], in0=xt[0:npart, blk, :],
                                                          scalar=stat[0:npart, 16 + blk:17 + blk], in1=gb_[0:npart, :],
                                                          op0=ALU.mult, op1=ALU.mult), r=[xtb, statb, cb], w=[xbl[0]])
            for half in range(2):
                bk = pb[half]
                for kk in range(4):
                    kc = half * 4 + kk
                    cx.op("pe", lambda e: e.transpose(out=banks[bk][:, kk * 128:kk * 128 + npart],
                                                      in_=xl[0][0:npart, kc * 128:(kc + 1) * 128],
                                                      identity=ident[0:npart, 0:npart]),
                          r=[xbl[0], cb], w=[bankb[bk]])
                src = banks[bk][:, :].rearrange("p (k t) -> p k t", k=4)[:, :, 0:npart]
                dst = xnT[:, half * 4:half * 4 + 4, blk * npart:(blk + 1) * npart]
                cp("act" if half == 0 else "dve", dst, src, [bankb[bk]], [xnTb])

    if 1 in phases:
        mark1 = sb.top
        wres = sb.alloc([128, 7, NKC, 512], BF16, "wres")
        wresb = Buf()
        for i, c0 in enumerate([C_DK, C_DK + 512, C_DV, C_DV + 512, C_RK, C_RV, C_RV + 512]):
            load_slab(wres[:, i], wresb, wi_s, c0, 512, q="sp")
        xts = [sb.alloc([128, 4, D], F32, "xt") for _ in range(2)]
        xtb = [Buf(), Buf()]
        xnT2 = [sb.alloc([128, NKC, TS], BF16, "xnT") for _ in range(2)]
        xnT2b = [Buf(), Buf()]
        xn_tmp = [sb.alloc([128, D], F32, "xn_tmp") for _ in range(4)]
        xn_tmpb = [Buf() for _ in range(4)]
        junk = sb.alloc([128, D], BF16, "junk")
        junkb = Buf()
        stat = sb.alloc([128, 24], F32, "stat")
        statb = Buf()
        ktile = [sb.alloc([128, TS], BF16, "ktile") for _ in range(8)]
        ktb = [Buf() for _ in range(8)]
        stgf = [sb.alloc([128, D], F32, "stgf") for _ in range(4)]
        stgb = [Buf() for _ in range(4)]
        vbf = [sb.alloc([128, D], BF16, "vbf") for _ in range(4)]
        vbb = [Buf() for _ in range(4)]
        rott = [sb.alloc([128, 4, 128], F32, "rott") for _ in range(2)]
        rotb = [Buf(), Buf()]
        kdt = sb.alloc([128, 4, RH], F32, "kdt")
        kdm = sb.alloc([NMETA, RH], F32, "kdm")
        kdb = Buf()
        cx.dma("sp", kdt[:, :, :], kdec_t[:, :].rearrange("(b p) h -> p b h", p=128), w=[kdb])
        cx.dma("sp", kdm[:, :], kdec_m[:, :], w=[kdb])
        rtmp = [sb.alloc([128, RH, 64], F32, "rtmp") for _ in range(4)]
        rtb = Buf()
        krot = sb.alloc([128, RH, 128], F32, "krot")
        krotb = Buf()
        krd = sb.alloc([128, 4, RH * RDK], BF16, "krd")
        krdb = Buf()
        rvb = sb.alloc([128, 4, RH * RDV], BF16, "rvb")
        rvbb = Buf()
        S = sb.alloc([128, RH, RDV], F32, "S")
        Sb_ = Buf()
        snapt = sb.alloc([128, RH, RDV], BF16, "snapt")
        snapb = Buf()
        cx.op("dve", lambda e: e.memset(S[:, :, :], 0.0), w=[Sb_])

        tiles = [("meta", 0)] + [("frame", t) for t in range(NT)]
        if 3 in phases:
            tiles += [("samp", 0), ("samp", 1)]
        sti = 0
        deferred_su = []
        for ti, (kind, t) in enumerate(tiles):
            npart, nblk = (128, 4) if kind == "frame" else (NMETA, 1)
            N = npart * nblk
            tok0 = NMETA + t * TS if kind == "frame" else 0
            xs = ti % 2

            def load_x(ti_):
                kind_, t_ = tiles[ti_]
                xs_ = ti_ % 2
                if kind_ == "meta":
                    cx.dma("sp", xts[xs_][0:NMETA, 0, :], meta[:, :], w=[xtb[xs_]])
                elif kind_ == "samp":
                    cx.dma("sp", xts[xs_][0:16, 0, :], xs_in[t_, :, :], w=[xtb[xs_]])
                else:
                    cx.dma("sp", xts[xs_][:, :, :], x_all[t_ * TS:(t_ + 1) * TS, :].rearrange("(b p) d -> p b d", p=128), w=[xtb[xs_]])

            def load_rot(ti_):
                kind_, t_ = tiles[ti_]
                rs2 = ti_ % 2
                if kind_ == "samp":
                    cx.dma("sp", rott[rs2][0:16, 0, :], rot_tm_s[:, :], w=[rotb[rs2]])
                elif kind_ == "meta":
                    cx.dma("sp", rott[rs2][0:NMETA, 0, :], rot_tm[0:NMETA, :], w=[rotb[rs2]])
                else:
                    r0_ = NMETA + t_ * TS
                    cx.dma("sp", rott[rs2][:, :, :], rot_tm[r0_:r0_ + TS, :].rearrange("(b p) f -> p b f", p=128), w=[rotb[rs2]])

            if ti == 0:
                load_x(0)
                load_rot(0)
            if ti + 1 < len(tiles):
                load_x(ti + 1)
                load_rot(ti + 1)
            if kind == "samp":
                while len(deferred_su) > 0:
                    deferred_su.pop(0)()
                cx.dma("sp", S[:, :, :], st_in[t, :, :, :].rearrange("h p e -> p h e"), w=[Sb_])
                cx.op("act", lambda e: e.copy(out=snapt[:, :, :], in_=S[:, :, :]), r=[Sb_], w=[snapb])
                cx.dma("sp", snap_s[t, :, :, :].rearrange("h p e -> p h e"), snapt[:, :, :], r=[snapb])
            xnT, xnTb = xnT2[ti % 2], xnT2b[ti % 2]

            def tile_dims(ti_):
                return (128, 4) if tiles[ti_][0] == "frame" else (NMETA, 1)

            if ti == 0:
                rms_scale(xts[xs], xtb[xs], npart, nblk, xn_tmp, xn_tmpb, junk, junkb, stat, statb, soff=0)
                rms_transp(npart, nblk, xnT, xnTb, xn_tmp, xn_tmpb, (0, 1))
            for h in range(DH_):
                bk = 2 + (h % 2)
                for kc in range(NKC):
                    cx.op("pe", lambda e: e.matmul(banks[bk][:, 0:N], lhsT=wres[:, h // 4, kc, (h % 4) * 128:(h % 4 + 1) * 128],
                                                   rhs=xnT[:, kc, 0:N], start=(kc == 0), stop=(kc == NKC - 1)),
                          r=[wresb, xnTb], w=[bankb[bk]])
                ks = h
                cp("act" if h % 2 == 0 else "dve", ktile[ks][:, 0:N], banks[bk][:, 0:N], [bankb[bk]], [ktb[ks]])
                if kind == "samp":
                    cx.dma("sp", KTn[t, h, :, :], ktile[ks][:, 0:N], r=[ktb[ks]])
                else:
                    cx.dma("sp", KTs[h, :, tok0:tok0 + N], ktile[ks][:, 0:N], r=[ktb[ks]])
            while len(deferred_su) > 0:
                deferred_su.pop(0)()
            if ti + 1 < len(tiles):
                np1, nb1 = tile_dims(ti + 1)
                xs1 = (ti + 1) % 2
                rms_scale(xts[xs1], xtb[xs1], np1, nb1, xn_tmp, xn_tmpb, junk, junkb, stat, statb, soff=4 * ((ti + 1) % 2))
            for blk in range(nblk):
                lo = blk * npart
                r0 = tok0 + lo
                if blk == min(1, nblk - 1) and ti + 1 < len(tiles):
                    rms_transp(np1, nb1, xnT2[(ti + 1) % 2], xnT2b[(ti + 1) % 2], xn_tmp, xn_tmpb, (0, 1))

                def tm_mm(bk, slab_i):
                    for kc in range(NKC):
                        cx.op("pe", lambda e: e.matmul(banks[bk][0:npart, :], lhsT=xnT[:, kc, lo:lo + npart], rhs=wres[:, slab_i, kc, :],
                                                       start=(kc == 0), stop=(kc == NKC - 1)), r=[wresb, xnTb], w=[bankb[bk]])
                kdst = k_rows_s[t * 16:(t + 1) * 16, :] if kind == "samp" else k_rows[r0:r0 + npart, :]
                vdst = v_rows_s[t * 16:(t + 1) * 16, :] if kind == "samp" else v_rows[r0:r0 + npart, :]
                s = sti % 4
                sti += 1
                for half in range(2):
                    bk = 4 + half
                    tm_mm(bk, half)
                    cp("act" if half == 0 else "dve", stgf[s][0:npart, half * 512:(half + 1) * 512], banks[bk][0:npart, :],
                       [bankb[bk]], [stgb[s]])
                cx.dma("sp", kdst, stgf[s][0:npart, :], r=[stgb[s]], is_out=True)
                s = sti % 4
                sti += 1
                vs = blk % 4
                for half in range(2):
                    bk = 6 + half
                    tm_mm(bk, 2 + half)
                    cp("act" if half == 0 else "dve", stgf[s][0:npart, half * 512:(half + 1) * 512], banks[bk][0:npart, :],
                       [bankb[bk]], [stgb[s]])
                cx.op("act", lambda e: e.copy(out=vbf[vs][0:npart, :], in_=stgf[s][0:npart, :]), r=[stgb[s]], w=[vbb[vs]])
                cx.dma("sp", vdst, stgf[s][0:npart, :], r=[stgb[s]], is_out=True)
                vsrc = vbf[vs][0:npart, :].rearrange("p (h e) -> p h e", h=DH_)
                if kind == "meta":
                    cx.dma("sp", Vms[:, :, :].rearrange("h p e -> p h e"), vsrc, r=[vbb[vs]])
                elif kind == "samp":
                    cx.dma("sp", Vn[t, :, :, :].rearrange("h p e -> p h e"), vsrc, r=[vbb[vs]])
                else:
                    cx.dma("sp", Vs[:, :, t * 4 + blk, :].rearrange("h p e -> p h e"), vsrc, r=[vbb[vs]])
                rs_ = ti % 2
                bk = blk % 2
                tm_mm(bk, 4)
                for hh in range(RH):
                    cosr = rott[rs_][0:npart, blk, 0:64]
                    sinr = rott[rs_][0:npart, blk, 64:128]
                    x1 = banks[bk][0:npart, hh * 128:hh * 128 + 64]
                    x2 = banks[bk][0:npart, hh * 128 + 64:hh * 128 + 128]
                    for ri, (xa, tb_) in enumerate([(x1, cosr), (x2, sinr), (x1, sinr), (x2, cosr)]):
                        cx.op("dve", lambda e: e.tensor_tensor(out=rtmp[ri][0:npart, hh, :], in0=xa, in1=tb_, op=ALU.mult),
                              r=[bankb[bk], rotb[rs_]], w=[rtb])
                cx.op("dve", lambda e: e.tensor_tensor(out=krot[0:npart, :, 0:64], in0=rtmp[0][0:npart, :, :], in1=rtmp[1][0:npart, :, :],
                                                       op=ALU.subtract), r=[rtb], w=[krotb])
                cx.op("dve", lambda e: e.tensor_tensor(out=krot[0:npart, :, 64:128], in0=rtmp[2][0:npart, :, :], in1=rtmp[3][0:npart, :, :],
                                                       op=ALU.add), r=[rtb], w=[krotb])
                for hh in range(RH):
                    kd = kdt[:, blk, hh:hh + 1] if kind == "frame" else kdm[0:NMETA, hh:hh + 1]
                    cx.op("act", lambda e: e.activation(out=krd[0:npart, blk, hh * 128:(hh + 1) * 128], in_=krot[0:npart, hh, :],
                                                        func=AF.Copy, scale=kd), r=[krotb, kdb], w=[krdb])
                for half in range(2):
                    bk = 6 + half
                    tm_mm(bk, 5 + half)
                    cp("act" if half == 0 else "dve", rvb[0:npart, blk, half * 512:(half + 1) * 512], banks[bk][0:npart, :],
                       [bankb[bk]], [rvbb])
            def state_update(kind=kind, t=t, npart=npart, nblk=nblk, N=N):
                for hh in range(RH):
                    bk = 2 + (hh % 2)
                    for blk in range(nblk):
                        cx.op("pe", lambda e: e.matmul(banks[bk][:, 0:RDV], lhsT=krd[0:npart, blk, hh * 128:(hh + 1) * 128],
                                                       rhs=rvb[0:npart, blk, hh * RDV:(hh + 1) * RDV], start=(blk == 0), stop=(blk == nblk - 1)),
                              r=[krdb, rvbb], w=[bankb[bk]])
                    gN = GAM[hh] ** N
                    cx.op("dve", lambda e: e.scalar_tensor_tensor(out=S[:, hh, :], in0=S[:, hh, :], scalar=float(gN), in1=banks[bk][:, 0:RDV],
                                                                  op0=ALU.mult, op1=ALU.add), r=[bankb[bk], Sb_], w=[Sb_])
                if kind == "samp":
                    cx.dma("sp", ret_state_s[t, :, :, :].rearrange("h p e -> p h e"), S[:, :, :], r=[Sb_], is_out=True)
                else:
                    cx.op("act", lambda e: e.copy(out=snapt[:, :, :], in_=S[:, :, :]), r=[Sb_], w=[snapb])
                    sidx = 0 if kind == "meta" else t + 1
                    cx.dma("sp", snaps[sidx, :, :, :].rearrange("h p e -> p h e"), snapt[:, :, :], r=[snapb])
                    if kind == "frame" and t == NT - 1:
                        cx.dma("sp", ret_state[:, :, :].rearrange("h p e -> p h e"), S[:, :, :], r=[Sb_], is_out=True)

            deferred_su.append(state_update)
        while len(deferred_su) > 0:
            deferred_su.pop(0)()
        cx.barrier()
        sb.top = mark0

    if 2 in phases or 3 in phases:
        markS = sb.top
        relf = sb.alloc([32, DH_], F32, "relf")
        relbb = sb.alloc([32, DH_], BF16, "relbb")
        ohf = sb.alloc([32, 1664], F32, "ohf")
        ohb = sb.alloc([32, 1664], BF16, "ohb")
        frt = sb.alloc([DH_, 1664], BF16, "frt")
        sb_ = Buf()
        cx.dma("sp", relf[:, :], relb[:, :], w=[sb_])
        cx.op("dve", lambda e: e.tensor_copy(out=relbb[:, :], in_=relf[:, :]), r=[sb_], w=[sb_])
        for (src, dst, n) in ((ohr, FRs, 1664), (ohs, FRss, 320)):
            cx.dma("sp", ohf[:, 0:n], src[:, :], r=[sb_], w=[sb_])
            cx.op("dve", lambda e: e.tensor_copy(out=ohb[:, 0:n], in_=ohf[:, 0:n]), r=[sb_], w=[sb_])
            for c0 in range(0, n, 416):
                cw = min(416, n - c0)
                cx.op("pe", lambda e: e.matmul(banks[0][0:DH_, 0:cw], lhsT=relbb[:, :], rhs=ohb[:, c0:c0 + cw], start=True, stop=True),
                      r=[sb_], w=[bankb[0]])
                cx.op("dve", lambda e: e.tensor_copy(out=frt[:, c0:c0 + cw], in_=banks[0][0:DH_, 0:cw]), r=[bankb[0]], w=[sb_])
            cx.dma("sp", dst[:, :], frt[:, 0:n], r=[sb_], w=[sb_])
        msk = sb.alloc([128, 9, TS], F32, "msk")
        mskb = Buf()
        cx.dma("sp", msk[:, :, :], mask8[:, :, :], w=[mskb])
        Ht = [sb.alloc([128, TS], BF16, "Ht") for _ in range(2)]
        Htb = [Buf(), Buf()]
        Tt = [sb.alloc([128, TS], BF16, "Tt") for _ in range(2)]
        Ttb = [Buf(), Buf()]
        it = 0

        def hank(t, off, npart, n):
            return bass.AP(tensor=t, offset=off, ap=[[1, npart], [1, n]])

        for h in range(DH_):
            jobs = []
            if 2 in phases:
                jobs += [("w", p) for p in range(9)] + [("m", 0)]
            if 3 in phases:
                jobs += [("s1", 0), ("s2", 0)]
            for (kind, p) in jobs:
                s = it % 2
                bk = it % 2
                it += 1
                if kind == "w":
                    npk, n, src, jm = 128, TS, hank(FRs, h * 1664 + 1024 - 128 * p, 128, TS), jmatb[:, :]
                elif kind == "m":
                    npk, n, src, jm = 16, TS, hank(FRs, h * 1664 + 1024, 16, TS), j16b[:, :]
                elif kind == "s1":
                    npk, n, src, jm = 128, 16, hank(FRss, h * 320 + 160, 128, 16), jmatb[:, :]
                else:
                    npk, n, src, jm = 16, 16, hank(FRss, h * 320 + 144, 16, 16), j16b[:, :]
                cx.dma("sp", Ht[s][0:npk, 0:n], src, r=[sb_], w=[Htb[s]])
                cx.op("pe", lambda e: e.matmul(banks[bk][0:npk, 0:n], lhsT=jm, rhs=Ht[s][0:npk, 0:n], start=True, stop=True),
                      r=[Htb[s], cb], w=[bankb[bk]])
                if kind == "w":
                    cx.op("dve", lambda e: e.scalar_tensor_tensor(out=Tt[s][:, :], in0=banks[bk][:, :], scalar=8.0, in1=msk[:, p, :],
                                                                  op0=ALU.mult, op1=ALU.add), r=[bankb[bk], mskb], w=[Ttb[s]])
                    cx.dma("sp", Ts[h, :, p, :], Tt[s][:, :], r=[Ttb[s]])
                else:
                    cx.op("dve", lambda e: e.tensor_scalar(out=Tt[s][0:npk, 0:n], in0=banks[bk][0:npk, 0:n], scalar1=8.0, scalar2=None,
                                                           op0=ALU.mult), r=[bankb[bk]], w=[Ttb[s]])
                    dst = {"m": Tms[h, :, :], "s1": Tss[h, :, :], "s2": Tsn[h, :, :]}[kind]
                    cx.dma("sp", dst, Tt[s][0:npk, 0:n], r=[Ttb[s]])
        cx.barrier()
        sb.top = markS

        xnT = sb.alloc([128, NKC, TS], BF16, "xnT")
        xnTb = Buf()
        QT = sb.alloc([128, DH_, TS], BF16, "QT")
        QTb = Buf()
        yretT = sb.alloc([128, 8, TS], BF16, "yretT")
        yretb = Buf()
        ydiffT = sb.alloc([128, 8, TS], BF16, "ydiffT")
        ydiffb = Buf()
        xn_tmp = sb.alloc([128, D], F32, "xn_tmp")
        xn_tmpb = Buf()
        junk = sb.alloc([128, D], BF16, "junk")
        junkb = Buf()
        stat = sb.alloc([128, 24], F32, "stat")
        statb = Buf()
        rrow = sb.alloc([64, 4, TS], F32, "rrow")
        rrowb = Buf()
        bchi = sb.alloc([64, TS], BF16, "bchi")
        bclo = sb.alloc([64, TS], BF16, "bclo")
        bcb = Buf()
        xtA = sb.alloc([128, 4, D], F32, "xtA")
        xtAb = Buf()
        markQ = sb.top
        bank_ctr = [0]

        def nbank(lo=0, hi=8):
            b = lo + bank_ctr[0] % (hi - lo)
            bank_ctr[0] += 1
            return b

        def bcast_prep(row, rb, n, p0=0):
            hi, lo = bchi[p0:p0 + 1, 0:n], bclo[p0:p0 + 1, 0:n]
            cx.op("dve", lambda e: e.tensor_copy(out=hi, in_=row), r=rb, w=[bcb])
            cx.op("dve", lambda e: e.tensor_tensor(out=lo, in0=row, in1=hi, op=ALU.subtract), r=rb + [bcb], w=[bcb])

        def bcast_pe(n, bank, p0=0):
            hi, lo = bchi[p0:p0 + 1, 0:n], bclo[p0:p0 + 1, 0:n]
            cx.op("pe", lambda e: e.matmul(banks[bank][:, 0:n], lhsT=onesb[p0:p0 + 1, 0:128], rhs=hi, start=True, stop=False),
                  r=[bcb, cb], w=[bankb[bank]])
            cx.op("pe", lambda e: e.matmul(banks[bank][:, 0:n], lhsT=onesb[p0:p0 + 1, 0:128], rhs=lo, start=False, stop=True),
                  r=[bcb, cb], w=[bankb[bank]])

        def bcast_row(row, rb, n, bank, p0=0):
            bcast_prep(row, rb, n, p0)
            bcast_pe(n, bank, p0)

        class Ring:
            def __init__(self, shape, dt, n, name):
                self.t = [sb.alloc(shape, dt, name) for _ in range(n)]
                self.b = [Buf() for _ in range(n)]
                self.i = 0

            def nxt(self):
                k = self.i % len(self.t)
                self.i += 1
                return self.t[k], self.b[k]

        def fm_chunk(slab, slb, oc, N, bank):
            for kc in range(NKC):
                cx.op("pe", lambda e: e.matmul(banks[bank][:, 0:N], lhsT=slab[:, kc, oc * 128:(oc + 1) * 128], rhs=xnT[:, kc, 0:N],
                                               start=(kc == 0), stop=(kc == NKC - 1)), r=[slb, xnTb], w=[bankb[bank]])

        def ret_unit(h, Qr, Qrd, Kr, qkb, n, kbs, Sin, Sinb, srg_aps, srgb, out_aps, outb, T_, bs=0):
            ob0 = 2 * bs
            sb0 = 4 + 2 * bs
            sds = []
            for i, kb in enumerate(kbs):
                bk = sb0 + i % 2
                npk, k0, q0 = kb["npk"], kb["k0"], kb["q0"]
                cx.op("pe", lambda e: e.matmul(banks[bk][0:npk, 0:n - q0], lhsT=Kr[:, k0:k0 + npk], rhs=Qr[:, q0:n], start=True, stop=True),
                      r=[qkb], w=[bankb[bk]])
                sd, sdb = T_["sD"][bs].t[i], T_["sD"][bs].b[i]
                cx.op("dve", lambda e: e.tensor_tensor(out=sd[0:npk, 0:n - q0], in0=banks[bk][0:npk, 0:n - q0], in1=kb["dt"], op=ALU.mult),
                      r=[bankb[bk], kb["dtb"]], w=[sdb])
                sds.append((sd, sdb))
            yield
            for dvc in range(2):
                cx.op("pe", lambda e: e.matmul(banks[ob0 + dvc][:, 0:n], lhsT=Sin[:, dvc * 128:(dvc + 1) * 128], rhs=Qrd[:, 0:n],
                                               start=True, stop=False), r=[Sinb, qkb], w=[bankb[ob0 + dvc]])
                for i, kb in enumerate(kbs):
                    npk, q0 = kb["npk"], kb["q0"]
                    cx.op("pe", lambda e: e.matmul(banks[ob0 + dvc][:, q0:n], lhsT=kb["v"][:, dvc * 128:(dvc + 1) * 128],
                                                   rhs=sds[i][0][0:npk, 0:n - q0], start=False, stop=(i == len(kbs) - 1)),
                          r=[kb["vb"], sds[i][1]], w=[bankb[ob0 + dvc]])
            yield
            sqs = []
            for dvc in range(2):
                sq, sqb = T_["sq"][bs].nxt()
                cx.op("act", lambda e: e.activation(out=sq[:, 0:n], in_=banks[ob0 + dvc][:, 0:n], func=AF.Square), r=[bankb[ob0 + dvc]], w=[sqb])
                sqs.append((sq, sqb))
            yield
            for dvc in range(2):
                sq, sqb = sqs[dvc]
                cx.op("pe", lambda e: e.matmul(banks[sb0][0:1, 0:n], lhsT=onesb[:, 0:1], rhs=sq[:, 0:n], start=(dvc == 0), stop=(dvc == 1)),
                      r=[sqb, cb], w=[bankb[sb0]])
            yield
            rr = T_["rr"][bs]
            rrb = T_["rrb"][bs]
            cx.op("act", lambda e: e.activation(out=rr[0:1, 0, 0:n], in_=banks[sb0][0:1, 0:n], func=AF.Sqrt, bias=eps6[0:1, :],
                                                scale=1.0 / RDV), r=[bankb[sb0], cb], w=[rrb])
            cx.op("dve", lambda e: e.reciprocal(out=rr[0:1, 1, 0:n], in_=rr[0:1, 0, 0:n]), r=[rrb], w=[rrb])
            hi, lo = rr[0:1, 2, 0:n], rr[0:1, 3, 0:n]
            hib, lob = T_["rh"][bs][0:1, 0, 0:n], T_["rh"][bs][0:1, 1, 0:n]
            cx.op("dve", lambda e: e.tensor_copy(out=hib, in_=rr[0:1, 1, 0:n]), r=[rrb], w=[rrb])
            cx.op("dve", lambda e: e.tensor_tensor(out=lob, in0=rr[0:1, 1, 0:n], in1=hib, op=ALU.subtract), r=[rrb], w=[rrb])
            yield
            cx.op("pe", lambda e: e.matmul(banks[sb0 + 1][:, 0:n], lhsT=onesb[0:1, 0:128], rhs=hib, start=True, stop=False),
                  r=[rrb, cb], w=[bankb[sb0 + 1]])
            cx.op("pe", lambda e: e.matmul(banks[sb0 + 1][:, 0:n], lhsT=onesb[0:1, 0:128], rhs=lob, start=False, stop=True),
                  r=[rrb, cb], w=[bankb[sb0 + 1]])
            yield
            Rb, Rbb = T_["f32"][bs].nxt()
            cx.op("act", lambda e: e.copy(out=Rb[:, 0:n], in_=banks[sb0 + 1][:, 0:n]), r=[bankb[sb0 + 1]], w=[Rbb])
            yield
            for dvc in range(2):
                tmp, tmpb = T_["f32"][bs].nxt()
                cx.op("dve", lambda e: e.tensor_tensor(out=tmp[:, 0:n], in0=banks[ob0 + dvc][:, 0:n], in1=Rb[:, 0:n], op=ALU.mult),
                      r=[bankb[ob0 + dvc], Rbb], w=[tmpb])
                cx.op("pool", lambda e: e.tensor_tensor(out=out_aps[dvc], in0=tmp[:, 0:n], in1=srg_aps[dvc], op=ALU.mult),
                      r=[tmpb, srgb], w=[outb])

        def run_interleaved(makers):
            active = {}
            pend = list(makers)
            for bs in (0, 1):
                if pend:
                    active[bs] = pend.pop(0)(bs)
            while active:
                for bs in list(active.keys()):
                    try:
                        next(active[bs])
                    except StopIteration:
                        if pend:
                            active[bs] = pend.pop(0)(bs)
                        else:
                            del active[bs]

        def ret_bufs(T_, n):
            T_["sD"] = [Ring([128, n], BF16, 4, "sD") for _ in range(2)]
            T_["sq"] = [Ring([128, n], BF16, 2, "sq") for _ in range(2)]
            T_["f32"] = [Ring([128, n], F32, 3, "rf32") for _ in range(2)]
            T_["rr"] = [sb.alloc([1, 4, n], F32, "rr") for _ in range(2)]
            T_["rrb"] = [Buf(), Buf()]
            T_["rh"] = [sb.alloc([1, 2, n], BF16, "rh") for _ in range(2)]

        pending = []

        def run_pending(nmax=None):
            k = 0
            while pending and (nmax is None or k < nmax):
                pending.pop(0)()
                k += 1

        def attn_unit(qT, qTb, n, blocks, out_ap, outb, T_):
            nb = len(blocks)
            pts = {}
            use_zacc = False
            nz_last = nb - 5
            if use_zacc:
                zacc, zab, zh, zl, zhb = T_["zacc"], T_["zab"], T_["zh"], T_["zl"], T_["zhb"]
                cx.op("dve", lambda e: e.memset(zacc[0][:, 0:n], 0.0), w=[zab[0]])
                cx.op("pool", lambda e: e.memset(zacc[1][:, 0:n], 0.0), w=[zab[1]])

            def emit_st(bi):
                bl = blocks[bi]
                npk = bl["npk"]
                for c in range(2):
                    bk = 4 + (bi % 2) * 2 + c
                    cx.op("pe", lambda e: e.matmul(banks[bk][0:npk, 0:n], lhsT=bl["kT"][c * 64:(c + 1) * 64, :],
                                                   rhs=qT[c * 64:(c + 1) * 64, 0:n], start=True, stop=(bl["T"] is None)),
                          r=[bl["kb"], qTb], w=[bankb[bk]])
                for c in range(2):
                    bk = 4 + (bi % 2) * 2 + c
                    if bl["T"] is not None:
                        cx.op("pe", lambda e: e.matmul(banks[bk][0:npk, 0:n], lhsT=identb[0:npk, 0:npk], rhs=bl["T"], start=False, stop=True),
                              r=[bl["Tb"], cb], w=[bankb[bk]])
                pr = 2 + (bi % 2)
                pt2, ptb = T_["PT"].nxt()
                src2 = pairs[pr][0:npk, :].rearrange("p (c m) -> p c m", c=2)[:, :, 0:n]
                cx.op("act", lambda e: e.activation(out=pt2[0:npk, :, 0:n], in_=src2, func=AF.Exp, bias=bl["bias"][0:npk, :], scale=0.125),
                      r=[bankb[2 * pr], bankb[2 * pr + 1], cb], w=[ptb])
                for c in range(2):
                    pts[(bi, c)] = (pt2[:, c, :], ptb)

            def emit_av(bi):
                bl = blocks[bi]
                npk = bl["npk"]
                pp = [pts.pop((bi, c)) for c in range(2)]
                for c in range(2):
                    pt, ptb = pp[c]
                    cx.op("pe", lambda e: e.matmul(banks[c][:, 0:n], lhsT=bl["v"], rhs=pt[0:npk, 0:n], start=(bi == 0), stop=(bi == nb - 1)),
                          r=[bl["vb"], ptb], w=[bankb[c]])
                if bi % 2 == 0 or bi >= nz_last or not use_zacc:
                    for c in range(2):
                        pt, ptb = pp[c]
                        cx.op("pe", lambda e: e.matmul(banks[2][32 * c:32 * c + 1, 0:n], lhsT=onesb[0:npk, 0:1], rhs=pt[0:npk, 0:n],
                                                       start=(bi == 0), stop=(bi == nb - 1 and not use_zacc), tile_position=(0, 32 * c)),
                              r=[ptb, cb], w=[bankb[2]])
                else:
                    for c in range(2):
                        pt, ptb = pp[c]
                        cx.op("dve" if c == 0 else "pool",
                              lambda e: e.tensor_tensor(out=zacc[c][:, 0:n], in0=zacc[c][:, 0:n], in1=pt[:, 0:n], op=ALU.add),
                              r=[ptb, zab[c]], w=[zab[c]])
                if use_zacc and bi == nz_last:
                    for c in range(2):
                        ze = "dve" if c == 0 else "pool"
                        cx.op(ze, lambda e: e.tensor_copy(out=zh[c][:, 0:n], in_=zacc[c][:, 0:n]), r=[zab[c]], w=[zhb[c]])
                        cx.op(ze, lambda e: e.tensor_tensor(out=zl[c][:, 0:n], in0=zacc[c][:, 0:n], in1=zh[c][:, 0:n], op=ALU.subtract),
                              r=[zab[c], zhb[c]], w=[zhb[c]])
                if use_zacc and bi == nb - 1:
                    for zi, zt in enumerate((zh, zl)):
                        for c in range(2):
                            cx.op("pe", lambda e: e.matmul(banks[2][32 * c:32 * c + 1, 0:n], lhsT=onesb[:, 0:1], rhs=zt[c][:, 0:n],
                                                           start=False, stop=(zi == 1), tile_position=(0, 32 * c)),
                                  r=[zhb[c], cb], w=[bankb[2]])

            for bi in range(nb + 1):
                if bi < nb:
                    emit_st(bi)
                if bi >= 1:
                    emit_av(bi - 1)
                if bi >= 2 and (nb < 28 or bi % 2 == 0):
                    run_pending(1)
            run_pending()
            o0s, o1s, m0, a_ = T_["o0s"], T_["o1s"], T_["m0"], T_["a"]
            ob = T_["ob"]
            cx.op("dve", lambda e: e.tensor_copy(out=o0s[:, 0:n], in_=banks[0][:, 0:n]), r=[bankb[0]], w=[ob])
            cx.op("dve", lambda e: e.tensor_copy(out=o1s[:, 0:n], in_=banks[1][:, 0:n]), r=[bankb[1]], w=[ob])
            cx.op("dve", lambda e: e.tensor_copy(out=rrow[0:33, 2, 0:n], in_=banks[2][0:33, 0:n]), r=[bankb[2]], w=[rrowb])
            sq, sqb = T_["sq"].nxt()

            def s0c():
                cx.op("dve", lambda e: e.reciprocal(out=rrow[0:1, 0, 0:n], in_=rrow[0:1, 2, 0:n]), r=[rrowb], w=[rrowb])
                cx.op("dve", lambda e: e.reciprocal(out=rrow[32:33, 1, 0:n], in_=rrow[32:33, 2, 0:n]), r=[rrowb], w=[rrowb])
                cx.op("dve", lambda e: e.tensor_scalar(out=rrow[32:33, 1, 0:n], in0=rrow[32:33, 1, 0:n], scalar1=neglam[32:33, 0:1],
                                                       scalar2=None, op0=ALU.mult), r=[rrowb, cb], w=[rrowb])

            def s1():
                bcast_prep(rrow[0:1, 0, 0:n], [rrowb], n)

            def s1b():
                bcast_pe(n, 3)

            def s2():
                cx.op("dve", lambda e: e.tensor_tensor(out=m0[:, 0:n], in0=o0s[:, 0:n], in1=banks[3][:, 0:n], op=ALU.mult),
                      r=[ob, bankb[3]], w=[ob])

            def s3():
                bcast_prep(rrow[32:33, 1, 0:n], [rrowb], n, p0=32)

            def s3b():
                bcast_pe(n, 3, p0=32)

            def s4():
                cx.op("dve", lambda e: e.tensor_tensor(out=a_[:, 0:n], in0=o1s[:, 0:n], in1=banks[3][:, 0:n], op=ALU.mult),
                      r=[ob, bankb[3]], w=[ob])
                cx.op("pool", lambda e: e.tensor_tensor(out=a_[:, 0:n], in0=a_[:, 0:n], in1=m0[:, 0:n], op=ALU.add), r=[ob], w=[ob])
                cx.op("pool", lambda e: e.tensor_tensor(out=sq[:, 0:n], in0=a_[:, 0:n], in1=a_[:, 0:n], op=ALU.mult), r=[ob], w=[sqb])

            def s5():
                cx.op("pe", lambda e: e.matmul(banks[3][0:1, 0:n], lhsT=onesb[:, 0:1], rhs=sq[:, 0:n], start=True, stop=True),
                      r=[sqb, cb], w=[bankb[3]])

            def s6():
                cx.op("act", lambda e: e.activation(out=rrow[0:1, 3, 0:n], in_=banks[3][0:1, 0:n], func=AF.Ln, bias=eps5[0:1, :],
                                                    scale=1.0 / DDV), r=[bankb[3], cb], w=[rrowb])
                cx.op("act", lambda e: e.activation(out=rrow[0:1, 3, 0:n], in_=rrow[0:1, 3, 0:n], func=AF.Exp, bias=zcol[0:1, :],
                                                    scale=-0.5), r=[rrowb, cb], w=[rrowb])

            def s7():
                bcast_prep(rrow[0:1, 3, 0:n], [rrowb], n)

            def s7b():
                bcast_pe(n, 3)

            def s8():
                cx.op("dve", lambda e: e.scalar_tensor_tensor(out=out_ap, in0=a_[:, 0:n], scalar=subgs[:, 0:1], in1=banks[3][:, 0:n],
                                                              op0=ALU.mult, op1=ALU.mult), r=[ob, bankb[3], cb], w=[outb])

            pending.extend([s0c, s1, s1b, s2, s3, s3b, s4, s5, s6, s7, s7b, s8])

        def tail(N, tmb, xsrc_dma, out_dst, xt_given=None, after_wo=None):
            mk = sb.top
            nb_ = len(tmb)
            if xt_given is None:
                xt = sb.alloc([128, nb_, D], F32, "t_xt")
                xtb = Buf()
                xsrc_dma(xt, xtb)
            else:
                xt, xtb = xt_given
            slabs = Ring([128, NKC, 512], BF16, 4, "slab")
            sgr = sb.alloc([128, 8, N], BF16, "sgr")
            sgd = sb.alloc([128, 8, N], BF16, "sgd")
            sgb = Buf()
            mergedT = sb.alloc([128, 8, N], BF16, "mergedT")
            mgb = Buf()
            hh_ = sb.alloc([128, nb_, D], F32, "t_h")
            hb = Buf()
            hnT = sb.alloc([128, NKC, N], BF16, "hnT")
            hnTb = Buf()
            actT = sb.alloc([128, 22, N], BF16, "actT")
            actb = Buf()
            f32r = Ring([128, N], F32, 3, "t_f32")
            yst = Ring([128, D], F32, 2, "yst")
            for gi, (c0, dstg) in enumerate(((C_GR, sgr), (C_GR + 512, sgr), (C_GD, sgd), (C_GD + 512, sgd))):
                sl, slb = slabs.nxt()
                load_slab(sl, slb, wi_s, c0, 512)
                for oc in range(4):
                    bk = nbank()
                    fm_chunk(sl, slb, oc, N, bk)
                    cx.op("act", lambda e: e.activation(out=dstg[:, (gi % 2) * 4 + oc, 0:N], in_=banks[bk][:, 0:N], func=AF.Sigmoid,
                                                        bias=zcol[:, :], scale=1.0), r=[bankb[bk], cb], w=[sgb])
            for half in range(2):
                sr, srb = slabs.nxt()
                load_slab(sr, srb, wrb_s, half * 512, 512)
                sd_, sdb_ = slabs.nxt()
                load_slab(sd_, sdb_, wdb_s, half * 512, 512)
                for oc in range(4):
                    dc = half * 4 + oc
                    ba, bb_ = nbank(), nbank()
                    for fc in range(8):
                        cx.op("pe", lambda e: e.matmul(banks[ba][:, 0:N], lhsT=sr[:, fc, oc * 128:(oc + 1) * 128], rhs=yretT[:, fc, 0:N],
                                                       start=(fc == 0), stop=(fc == 7)), r=[srb, yretb], w=[bankb[ba]])
                    for fc in range(8):
                        cx.op("pe", lambda e: e.matmul(banks[bb_][:, 0:N], lhsT=sd_[:, fc, oc * 128:(oc + 1) * 128], rhs=ydiffT[:, fc, 0:N],
                                                       start=(fc == 0), stop=(fc == 7)), r=[sdb_, ydiffb], w=[bankb[bb_]])
                    m0, m0b = f32r.nxt()
                    cx.op("dve", lambda e: e.tensor_tensor(out=m0[:, 0:N], in0=banks[ba][:, 0:N], in1=sgr[:, dc, 0:N], op=ALU.mult),
                          r=[bankb[ba], sgb], w=[m0b])
                    m1, m1b = f32r.nxt()
                    cx.op("dve", lambda e: e.tensor_tensor(out=m1[:, 0:N], in0=banks[bb_][:, 0:N], in1=sgd[:, dc, 0:N], op=ALU.mult),
                          r=[bankb[bb_], sgb], w=[m1b])
                    cx.op("pool", lambda e: e.tensor_tensor(out=mergedT[:, dc, 0:N], in0=m0[:, 0:N], in1=m1[:, 0:N], op=ALU.add),
                          r=[m0b, m1b], w=[mgb])
            for cg in range(2):
                sl, slb = slabs.nxt()
                load_slab(sl, slb, wo_s, cg * 512, 512)
                for bi, (npart, col0) in enumerate(tmb):
                    bk = nbank()
                    for kc in range(NKC):
                        cx.op("pe", lambda e: e.matmul(banks[bk][0:npart, :], lhsT=mergedT[:, kc, col0:col0 + npart], rhs=sl[:, kc, :],
                                                       start=(kc == 0), stop=(kc == NKC - 1)), r=[slb, mgb], w=[bankb[bk]])
                    cx.op("dve", lambda e: e.tensor_tensor(out=hh_[0:npart, bi, cg * 512:(cg + 1) * 512],
                                                           in0=banks[bk][0:npart, :], in1=xt[0:npart, bi, cg * 512:(cg + 1) * 512], op=ALU.add),
                          r=[bankb[bk], xtb], w=[hb])
            if after_wo is not None:
                after_wo()
            npart0 = tmb[0][0]
            nx_ = min(nb_, 3)
            xnE = [xn_tmp] + [sb.alloc([128, D], F32, "xnE") for _ in range(nx_ - 1)]
            xnEb = [xn_tmpb] + [Buf() for _ in range(nx_ - 1)]
            rms_to_fm(hh_, hb, npart0, nb_, hnT, hnTb, xnE, xnEb, junk, junkb, stat, statb, (nbank(0, 4), nbank(4, 8)), gainb=g2b)
            for i4 in range(6):
                ncol = min(512, DFF - i4 * 512)
                sg_, sgb_ = slabs.nxt()
                load_slab(sg_, sgb_, wup_s, i4 * 512, ncol)
                su_, sub_ = slabs.nxt()
                load_slab(su_, sub_, wup_s, DFF + i4 * 512, ncol)
                for oc in range(ncol // 128):
                    i = i4 * 4 + oc
                    bg, bu = nbank(), nbank()
                    for kc in range(NKC):
                        cx.op("pe", lambda e: e.matmul(banks[bg][:, 0:N], lhsT=sg_[:, kc, oc * 128:(oc + 1) * 128], rhs=hnT[:, kc, 0:N],
                                                       start=(kc == 0), stop=(kc == NKC - 1)), r=[sgb_, hnTb], w=[bankb[bg]])
                    for kc in range(NKC):
                        cx.op("pe", lambda e: e.matmul(banks[bu][:, 0:N], lhsT=su_[:, kc, oc * 128:(oc + 1) * 128], rhs=hnT[:, kc, 0:N],
                                                       start=(kc == 0), stop=(kc == NKC - 1)), r=[sub_, hnTb], w=[bankb[bu]])
                    sg_t, sgtb = f32r.nxt()
                    cx.op("act", lambda e: e.activation(out=sg_t[:, 0:N], in_=banks[bg][:, 0:N], func=AF.Silu, bias=zcol[:, :], scale=1.0),
                          r=[bankb[bg], cb], w=[sgtb])
                    cx.op("dve", lambda e: e.tensor_tensor(out=actT[:, i, 0:N], in0=sg_t[:, 0:N], in1=banks[bu][:, 0:N], op=ALU.mult),
                          r=[sgtb, bankb[bu]], w=[actb])
            for cg in range(2):
                bks = [nbank(0, 4) if False else (bi % 4) for bi in range(nb_)]
                bks = [(cg * 4 + bi) % 8 for bi in range(nb_)]
                for kg in range(3):
                    nk_ = 8 if kg < 2 else 6
                    sl, slb = slabs.nxt()
                    load_slab(sl, slb, wdn_s, cg * 512, 512, nkc=nk_, k0=kg * 8)
                    for bi, (npart, col0) in enumerate(tmb):
                        for kk in range(nk_):
                            i = kg * 8 + kk
                            cx.op("pe", lambda e: e.matmul(banks[bks[bi]][0:npart, :], lhsT=actT[:, i, col0:col0 + npart], rhs=sl[:, kk, :],
                                                           start=(i == 0), stop=(i == 21)), r=[slb, actb], w=[bankb[bks[bi]]])
                for bi, (npart, col0) in enumerate(tmb):
                    cx.op("dve", lambda e: e.tensor_tensor(out=hh_[0:npart, bi, cg * 512:(cg + 1) * 512], in0=banks[bks[bi]][0:npart, :],
                                                           in1=hh_[0:npart, bi, cg * 512:(cg + 1) * 512], op=ALU.add),
                          r=[bankb[bks[bi]], hb], w=[hb])
            for bi, (npart, col0) in enumerate(tmb):
                rms_rows(hh_[0:npart, bi, :], npart, D, stat, statb, bi, junk, junkb, [hb], eps6)
                ys, ysb = yst.nxt()
                cx.op("dve", lambda e: e.scalar_tensor_tensor(out=ys[0:npart, :], in0=hh_[0:npart, bi, :], scalar=stat[0:npart, 16 + bi:17 + bi],
                                                              in1=normfb[0:npart, :], op0=ALU.mult, op1=ALU.mult), r=[hb, statb, cb], w=[ysb])
                cx.dma("sp", out_dst(bi), ys[0:npart, :], r=[ysb], is_out=True)
            cx.barrier()
            sb.top = mk

        def inproj_q(N, rot_src, qdec_src, tm_units, T_):
            slabs = T_["slabs"]
            rot = sb.alloc([128, 4, N], F32, "rot")
            rotb_ = Buf()
            cx.dma("sp", rot[:, :, :], rot_src, w=[rotb_])
            qd = sb.alloc([128, RH * N], F32, "qd")
            cx.dma("sp", qd[:, :], qdec_src, w=[rotb_])
            f32r = T_["f32"]
            for which, (c0, c0s) in enumerate(((C_RQ, C_RQS), (C_RK, C_RKS))):
                sa, sab = slabs.nxt()
                load_slab(sa, sab, wi_s, c0, 512)
                ss_, ssb = slabs.nxt()
                load_slab(ss_, ssb, wi_s, c0s, 512)
                for h in range(RH):
                    ba, bb_ = nbank(), nbank()
                    fm_chunk(sa, sab, h, N, ba)
                    fm_chunk(ss_, ssb, h, N, bb_)
                    t1, t1b = f32r.nxt()
                    cx.op("dve", lambda e: e.tensor_tensor(out=t1[:, 0:N], in0=banks[ba][:, 0:N], in1=rot[:, 2 * which, :], op=ALU.mult),
                          r=[bankb[ba], rotb_], w=[t1b])
                    t2, t2b = f32r.nxt()
                    cx.op("dve", lambda e: e.tensor_tensor(out=t2[:, 0:N], in0=banks[bb_][:, 0:N], in1=rot[:, 2 * which + 1, :], op=ALU.mult),
                          r=[bankb[bb_], rotb_], w=[t2b])
                    if which == 0:
                        cx.op("pool", lambda e: e.tensor_tensor(out=t1[:, 0:N], in0=t1[:, 0:N], in1=t2[:, 0:N], op=ALU.add),
                              r=[t1b, t2b], w=[t1b])
                        cx.op("act", lambda e: e.copy(out=T_["Qr"][:, h, 0:N], in_=t1[:, 0:N]), r=[t1b], w=[T_["qkb"]])
                        cx.op("pool", lambda e: e.tensor_tensor(out=T_["Qrd"][:, h, 0:N], in0=t1[:, 0:N], in1=qd[:, h * N:(h + 1) * N], op=ALU.mult),
                              r=[t1b, rotb_], w=[T_["qkb"]])
                    else:
                        cx.op("pool", lambda e: e.tensor_tensor(out=T_["Kr"][:, h, 0:N], in0=t1[:, 0:N], in1=t2[:, 0:N], op=ALU.add),
                              r=[t1b, t2b], w=[T_["qkb"]])
            for half in range(2):
                sl, slb = slabs.nxt()
                load_slab(sl, slb, wi_s, C_RV + half * 512, 512)
                for (npart, col0, dst, dstb) in tm_units:
                    bk = nbank()
                    for kc in range(NKC):
                        cx.op("pe", lambda e: e.matmul(banks[bk][0:npart, :], lhsT=xnT[:, kc, col0:col0 + npart], rhs=sl[:, kc, :],
                                                       start=(kc == 0), stop=(kc == NKC - 1)), r=[slb, xnTb], w=[bankb[bk]])
                    cp("act" if half == 0 else "dve", dst[0:npart, half * 512:(half + 1) * 512], banks[bk][0:npart, :], [bankb[bk]], [dstb])
            for half in range(2):
                sl, slb = slabs.nxt()
                load_slab(sl, slb, wi_s, C_RG + half * 512, 512)
                for oc in range(4):
                    bk = nbank()
                    fm_chunk(sl, slb, oc, N, bk)
                    cx.op("act", lambda e: e.activation(out=T_["srg"][:, half * 4 + oc, 0:N], in_=banks[bk][:, 0:N], func=AF.Silu,
                                                        bias=zcol[:, :], scale=1.0), r=[bankb[bk], cb], w=[T_["srgb"]])
            for half in range(2):
                sl, slb = slabs.nxt()
                load_slab(sl, slb, wi_s, C_DQ + half * 512, 512)
                for oc in range(4):
                    bk = nbank()
                    fm_chunk(sl, slb, oc, N, bk)
                    cp("dve" if oc % 2 else "act", QT[:, half * 4 + oc, 0:N], banks[bk][:, 0:N], [bankb[bk]], [QTb])

        if 2 in phases:
            for j in range(NOWN):
                N = TS
                if j == 0:
                    cx.dma("sp", xtA[:, :, :], x_own[0, :, :].rearrange("(b p) d -> p b d", p=128), w=[xtAb])
                mk = sb.top
                xnA = [xn_tmp] + [sb.alloc([128, D], F32, "xnA") for _ in range(3)]
                xnAb = [xn_tmpb] + [Buf() for _ in range(3)]
                rms_scale(xtA, xtAb, 128, 4, xnA, xnAb, junk, junkb, stat, statb, soff=0)
                rms_transp(128, 4, xnT, xnTb, xnA, xnAb, (0, 1))
                cx.barrier()
                sb.top = mk
                T_ = {}
                T_["Qr"] = sb.alloc([128, RH, N], BF16, "Qr")
                T_["Qrd"] = sb.alloc([128, RH, N], BF16, "Qrd")
                T_["Kr"] = sb.alloc([128, RH, N], BF16, "Kr")
                T_["qkb"] = Buf()
                T_["srg"] = sb.alloc([128, 8, N], BF16, "srg")
                T_["srgb"] = Buf()
                rvb = sb.alloc([128, 4, RH * RDV], BF16, "rvb2")
                rvbb = Buf()
                mkip = sb.top
                T_["slabs"] = Ring([128, NKC, 512], BF16, 4, "slab")
                T_["f32"] = Ring([128, N], F32, 4, "f32")
                tm_units = [(128, blk * 128, rvb[:, blk, :], rvbb) for blk in range(4)]
                inproj_q(N, rot_fm[j, :, :, :].rearrange("f p n -> p f n"), bc_ap(qdec, 0, RH * TS), tm_units, T_)
                cx.barrier()
                sb.top = mkip
                dtf = sb.alloc([128, 4, N], F32, "dtf")
                dtfb = Buf()
                dtb_ = sb.alloc([128, 16, N], BF16, "dtb")
                dtbb = Buf()
                for q4 in range(4):
                    cx.dma("sp", dtf[:, :, :], dt_tab[:, q4 * 4:(q4 + 1) * 4, :], w=[dtfb])
                    cp("act" if q4 % 2 == 0 else "dve", dtb_[:, q4 * 4:(q4 + 1) * 4, :], dtf[:, :, :], [dtfb], [dtbb])
                snA = sb.alloc([128, RH, RDV], BF16, "snA")
                snB = sb.alloc([128, RH, RDV], BF16, "snB")
                sin_ = sb.alloc([128, RH, RDV], BF16, "sin")
                snb_ = Buf()
                sinb = Buf()
                cx.dma("sp", snA[:, :, :], snaps[2 * j, :, :, :].rearrange("h p e -> p h e"), w=[snb_])
                cx.dma("sp", snB[:, :, :], snaps[2 * j + 1, :, :, :].rearrange("h p e -> p h e"), w=[snb_])
                cx.op("dve", lambda e: e.tensor_scalar(out=sin_[:, :, :], in0=snA[:, :, :], scalar1=selcs[:, 0:1], scalar2=None, op0=ALU.mult),
                      r=[snb_, cb], w=[sinb])
                cx.op("dve", lambda e: e.scalar_tensor_tensor(out=sin_[:, :, :], in0=snB[:, :, :], scalar=selcs[:, 1:2], in1=sin_[:, :, :],
                                                              op0=ALU.mult, op1=ALU.add), r=[snb_, cb, sinb], w=[sinb])
                TR = {}
                ret_bufs(TR, N)

                def mk_ret(h):
                    kbs = [dict(npk=128, k0=kb * 128, q0=kb * 128, v=rvb[:, kb, h * RDV:(h + 1) * RDV], vb=rvbb,
                                dt=dtb_[:, h * 4 + kb, kb * 128:N], dtb=dtbb) for kb in range(4)]
                    return lambda bs: ret_unit(h, T_["Qr"][:, h, :], T_["Qrd"][:, h, :], T_["Kr"][:, h, :], T_["qkb"], N, kbs,
                                               sin_[:, h, :], sinb, [T_["srg"][:, h * 2 + d_, 0:N] for d_ in range(2)], T_["srgb"],
                                               [yretT[:, h * 2 + d_, 0:N] for d_ in range(2)], yretb, TR, bs=bs)

                run_interleaved([mk_ret(h) for h in range(RH)])
                cx.barrier()
                sb.top = markQ
                nkt = 2 * j + 2
                nblk = 4 * nkt
                nk = NMETA + TS * nkt
                T_ = {}
                T_["PT"] = Ring([128, 2, N], BF16, 3, "PT")
                for nm_ in ("o0s", "o1s", "m0", "a"):
                    T_[nm_] = sb.alloc([128, N], F32, nm_)
                T_["ob"] = Buf()
                T_["sq"] = Ring([128, N], BF16, 2, "sq")
                KTr = Ring([128, nk], BF16, 2, "KTh")
                Vr = Ring([128, nblk, DDV], BF16, 2, "Vh")
                Vmr = Ring([NMETA, DDV], BF16, 2, "Vmh")
                Twr = Ring([128, 9, TS], BF16, 2, "Tw")
                Tmr = Ring([NMETA, TS], BF16, 2, "Tm")
                for h in range(DH_):
                    kt, ktb_ = KTr.nxt()
                    cx.dma("sp", kt[:, :], KTs[h, :, 0:nk], w=[ktb_])
                    vt, vtb = Vr.nxt()
                    cx.dma("sp", vt[:, :, :], Vs[h, :, 0:nblk, :], w=[vtb])
                    vm, vmb = Vmr.nxt()
                    cx.dma("sp", vm[:, :], Vms[h, :, :], w=[vmb])
                    tw, twb = Twr.nxt()
                    cx.dma("sp", tw[:, :, :], Ts[h, :, :, :], w=[twb])
                    tm_, tmb_ = Tmr.nxt()
                    if j == 0:
                        cx.dma("sp", tm_[:, :], Tms[h, :, :], w=[tmb_])
                    blocks = [dict(npk=NMETA, kT=kt[:, 0:NMETA], kb=ktb_, v=vm[:, :], vb=vmb,
                                   T=(tm_[:, :] if j == 0 else None), Tb=tmb_, bias=(zcol if j == 0 else cbias[:, h:h + 1]))]
                    for kb in range(nblk):
                        p = kb - (nblk - 9)
                        inwin = p >= 0
                        blocks.append(dict(npk=128, kT=kt[:, NMETA + kb * 128:NMETA + (kb + 1) * 128], kb=ktb_, v=vt[:, kb, :], vb=vtb,
                                           T=(tw[:, p, :] if inwin else None), Tb=twb, bias=(zcol if inwin else cbias[:, h:h + 1])))
                    attn_unit(QT[:, h, :], QTb, N, blocks, ydiffT[:, h, 0:N], ydiffb, T_)
                run_pending()
                cx.barrier()
                sb.top = markQ
                def prefetch_next(j=j):
                    if j + 1 < NOWN:
                        cx.dma("sp", xtA[:, :, :], x_own[j + 1, :, :].rearrange("(b p) d -> p b d", p=128), w=[xtAb])

                tail(N, [(128, blk * 128) for blk in range(4)], None,
                     lambda bi: y_own[j, bi * 128:(bi + 1) * 128, :], xt_given=(xtA, xtAb), after_wo=prefetch_next)

        if 3 in phases:
            N = 32
            mk = sb.top
            xt = sb.alloc([128, 1, D], F32, "xt")
            xtb = Buf()
            cx.dma("sp", xt[0:32, 0, :], xs_in[:, :, :].rearrange("s t d -> (s t) d"), w=[xtb])
            rms_to_fm(xt, xtb, 32, 1, xnT, xnTb, xn_tmp, xn_tmpb, junk, junkb, stat, statb, (0, 1))
            cx.barrier()
            sb.top = mk
            T_ = {}
            T_["Qr"] = sb.alloc([128, RH, N], BF16, "Qr")
            T_["Qrd"] = sb.alloc([128, RH, N], BF16, "Qrd")
            T_["Kr"] = sb.alloc([128, RH, N], BF16, "Kr")
            T_["qkb"] = Buf()
            T_["srg"] = sb.alloc([128, 8, N], BF16, "srg")
            T_["srgb"] = Buf()
            rvs = [sb.alloc([16, RH * RDV], BF16, "rvs") for _ in range(2)]
            rvsb = [Buf(), Buf()]
            mkip = sb.top
            T_["slabs"] = Ring([128, NKC, 512], BF16, 4, "slab")
            T_["f32"] = Ring([128, N], F32, 4, "f32")
            tm_units = [(16, s_ * 16, rvs[s_], rvsb[s_]) for s_ in range(2)]
            inproj_q(N, rot_fm_s[:, :, :].rearrange("f p n -> p f n"), bc_ap(qdec_s, 0, RH * 32), tm_units, T_)
            cx.barrier()
            sb.top = mkip
            dsf = sb.alloc([16, RH, 16], F32, "dsf")
            dsb = sb.alloc([16, RH, 16], BF16, "dsb")
            dsbb = Buf()
            cx.dma("sp", dsf[:, :, :], dt_s[:, :, :], w=[dsbb])
            cx.op("dve", lambda e: e.tensor_copy(out=dsb[:, :, :], in_=dsf[:, :, :]), r=[dsbb], w=[dsbb])
            sin_ = sb.alloc([128, RH, RDV], BF16, "sin")
            sinb = Buf()
            TR = {}
            ret_bufs(TR, 16)
            sins = [sb.alloc([128, RH, RDV], BF16, "sin") for _ in range(2)]
            sinbs = [Buf(), Buf()]
            makers = []
            for s_ in range(2):
                cx.dma("sp", sins[s_][:, :, :], snap_s[s_, :, :, :].rearrange("h p e -> p h e"), w=[sinbs[s_]])

                def mk_ret_s(h, s_=s_):
                    c0 = s_ * 16
                    kbs = [dict(npk=16, k0=0, q0=0, v=rvs[s_][:, h * RDV:(h + 1) * RDV], vb=rvsb[s_], dt=dsb[:, h, :], dtb=dsbb)]
                    return lambda bs: ret_unit(h, T_["Qr"][:, h, c0:c0 + 16], T_["Qrd"][:, h, c0:c0 + 16], T_["Kr"][:, h, c0:c0 + 16],
                                               T_["qkb"], 16, kbs, sins[s_][:, h, :], sinbs[s_],
                                               [T_["srg"][:, h * 2 + d_, c0:c0 + 16] for d_ in range(2)], T_["srgb"],
                                               [yretT[:, h * 2 + d_, c0:c0 + 16] for d_ in range(2)], yretb, TR, bs=bs)

                makers += [mk_ret_s(h) for h in range(RH)]
            run_interleaved(makers)
            cx.barrier()
            sb.top = markQ
            T_ = {}
            T_["PT"] = Ring([128, 2, 16], BF16, 3, "PT")
            for nm_ in ("o0s", "o1s", "m0", "a"):
                T_[nm_] = sb.alloc([128, 16], F32, nm_)
            T_["ob"] = Buf()
            T_["sq"] = Ring([128, 16], BF16, 2, "sq")
            KcT = sb.alloc([128, DH_, P], BF16, "KcT")
            KcTb = Buf()
            Vc = sb.alloc([128, PB, D], BF16, "Vc")
            Vcb = Buf()
            ckr = Ring([128, D], F32, 2, "ckr")
            KnT = sb.alloc([128, DH_, 16], BF16, "KnT")
            Vnn = sb.alloc([16, DH_, DDV], BF16, "Vnn")
            Ts1 = sb.alloc([128, DH_, 16], BF16, "Ts1")
            Ts2 = sb.alloc([16, DH_, 16], BF16, "Ts2")
            nwb = Buf()
            cx.dma("sp", Ts1[:, :, :], Tss[:, :, :].rearrange("h p q -> p h q"), w=[nwb])
            cx.dma("sp", Ts2[:, :, :], Tsn[:, :, :].rearrange("h p q -> p h q"), w=[nwb])
            for s_ in range(2):
                cx.dma("sp", KnT[:, :, :], KTn[s_, :, :, :].rearrange("h p q -> p h q"), w=[nwb])
                cx.dma("sp", Vnn[:, :, :], Vn[s_, :, :, :].rearrange("h p e -> p h e"), w=[nwb])
                for blk in range(PB):
                    ck, ckb = ckr.nxt()
                    cx.dma("sp", ck[:, :], ck_in[s_, blk * 128:(blk + 1) * 128, :], w=[ckb])
                    for half in range(2):
                        bk = nbank()
                        for hh in range(4):
                            h = half * 4 + hh
                            cx.op("pe", lambda e: e.transpose(out=banks[bk][:, hh * 128:(hh + 1) * 128], in_=ck[:, h * 128:(h + 1) * 128],
                                                              identity=ident[:, :]), r=[ckb, cb], w=[bankb[bk]])
                        cp("act" if half == 0 else "dve", KcT[:, half * 4:half * 4 + 4, blk * 128:(blk + 1) * 128],
                           banks[bk][:, :].rearrange("p (h t) -> p h t", h=4), [bankb[bk]], [KcTb])
                    cv, cvb = ckr.nxt()
                    cx.dma("sp", cv[:, :], cv_in[s_, blk * 128:(blk + 1) * 128, :], w=[cvb])
                    cx.op("pool", lambda e: e.tensor_copy(out=Vc[:, blk, :], in_=cv[:, :]), r=[cvb], w=[Vcb])
                c0 = s_ * 16
                for h in range(DH_):
                    blocks = []
                    for kb in range(PB):
                        last = kb == PB - 1
                        blocks.append(dict(npk=128, kT=KcT[:, h, kb * 128:(kb + 1) * 128], kb=KcTb, v=Vc[:, kb, h * DDV:(h + 1) * DDV], vb=Vcb,
                                           T=(Ts1[:, h, :] if last else None), Tb=nwb, bias=(zcol if last else cbias[:, h:h + 1])))
                    blocks.append(dict(npk=16, kT=KnT[:, h, :], kb=nwb, v=Vnn[:, h, :], vb=nwb, T=Ts2[:, h, :], Tb=nwb, bias=zcol))
                    attn_unit(QT[:, h, c0:c0 + 16], QTb, 16, blocks, ydiffT[:, h, c0:c0 + 16], ydiffb, T_)
            run_pending()
            cx.barrier()
            sb.top = markQ
            tail(32, [(32, 0)],
                 lambda xt_, xtb_: cx.dma("sp", xt_[0:32, 0, :], xs_in[:, :, :].rearrange("s t d -> (s t) d"), w=[xtb_]),
                 lambda bi: y_s[:, :])

    cx.finish()
    return nc, cx


def _rot_fm_tables(pos):
    half = 64
    freq = (1.0 / (10000.0 ** np.linspace(0.0, 1.0, half, dtype=np.float32))).astype(np.float32)
    ang = (np.asarray(pos, np.float32)[None, :] * freq[:, None]).astype(np.float32)
    c = np.concatenate([np.cos(ang), np.cos(ang)], axis=0)
    s = np.concatenate([-np.sin(ang), np.sin(ang)], axis=0)
    sc = np.float32(RDK ** -0.5)
    return np.stack([c * sc, s * sc, c, s]).astype(np.float32)


def _host_consts(SEQ, P):
    LTOK = NMETA + SEQ
    half = 64
    freq = (1.0 / (10000.0 ** np.linspace(0.0, 1.0, half, dtype=np.float32))).astype(np.float32)
    pos = np.concatenate([np.arange(-NMETA, 0), np.arange(SEQ)]).astype(np.float32)
    ang = (pos[:, None] * freq[None, :]).astype(np.float32)
    rot_tm = np.concatenate([np.cos(ang), np.sin(ang)], axis=1).astype(np.float32)
    pos_s = np.arange(P, P + 16).astype(np.float32)
    ang_s = (pos_s[:, None] * freq[None, :]).astype(np.float32)
    rot_tm_s = np.concatenate([np.cos(ang_s), np.sin(ang_s)], axis=1).astype(np.float32)
    rfs = _rot_fm_tables(pos_s)
    rot_fm_s = np.ascontiguousarray(np.concatenate([rfs, rfs], axis=2))
    i = np.arange(TS, dtype=np.float64)
    kdec_t = np.stack([np.exp(LG[h] * (TS - 1.0 - i)) for h in range(RH)], axis=1).astype(np.float32)
    im = np.arange(NMETA, dtype=np.float64)
    kdec_m = np.stack([np.exp(LG[h] * (NMETA - 1.0 - im)) for h in range(RH)], axis=1).astype(np.float32)
    qdec = np.concatenate([np.exp(LG[h] * (i + 1.0)) for h in range(RH)]).astype(np.float32)[None]
    qd16 = [np.exp(LG[h] * (im + 1.0)) for h in range(RH)]
    qdec_s = np.concatenate([np.concatenate([q, q]) for q in qd16]).astype(np.float32)[None]
    m = (np.arange(4)[None, :, None] * 128 + np.arange(128)[:, None, None]).astype(np.int64)
    n = np.arange(TS)[None, None, :].astype(np.int64)
    dt = np.zeros((128, 16, TS), np.float32)
    for h in range(RH):
        dcau = np.where(m <= n, np.exp(LG[h] * (n - m).astype(np.float64)), 0.0)
        dsame = np.where((m > n) & (m // CHUNK == n // CHUNK), np.exp(LG[h] * (m - n).astype(np.float64)), 0.0)
        dt[:, h * 4:(h + 1) * 4, :] = (dcau + dsame).astype(np.float32)
    a16 = np.abs(np.arange(16)[:, None] - np.arange(16)[None, :]).astype(np.float64)
    dt_s = np.stack([np.exp(LG[h] * a16) for h in range(RH)], axis=1).astype(np.float32)
    ident = np.eye(128, dtype=np.float32)
    jm = np.ascontiguousarray(ident[::-1])
    j16 = np.ascontiguousarray(np.eye(16, dtype=np.float32)[::-1])
    bs = t5_bucket_np(159 - np.arange(320))
    ohs = (bs[None, :] == np.arange(32)[:, None]).astype(np.float32)
    return dict(rot_tm=rot_tm, rot_tm_s=rot_tm_s, rot_fm_s=rot_fm_s, kdec_t=kdec_t, kdec_m=kdec_m, qdec=qdec, qdec_s=qdec_s,
                dt_tab=dt, dt_s=dt_s, c_ident=ident, c_j=jm, c_j16=j16, ohs=ohs)


def _core_consts(g, NOWN):
    rot = np.stack([_rot_fm_tables(np.arange((2 * j + g) * TS, (2 * j + g + 1) * TS)) for j in range(NOWN)])
    selc = np.zeros((128, 2), np.float32)
    selc[:, g] = 1.0
    bo = t5_bucket_np(1023 - 512 * g - np.arange(1664))
    ohr = (bo[None, :] == np.arange(32)[:, None]).astype(np.float32)
    k = np.arange(128)[:, None, None]
    p = np.arange(9)[None, :, None]
    q = np.arange(TS)[None, None, :]
    kr = (p - 1) * 128 - 512 * g + k
    vis = (kr // CHUNK) <= (q // CHUNK)
    mask8 = np.where(vis, 0.0, 8.0 * NEG).astype(np.float32)
    return dict(rot_fm=np.ascontiguousarray(rot), selc=selc, ohr=ohr, mask8=np.ascontiguousarray(mask8))


def _swap_halves_cols(w, c0, nheads, dk):
    cols = []
    for h in range(nheads):
        b = c0 + h * dk
        cols += list(range(b + dk // 2, b + dk)) + list(range(b, b + dk // 2))
    return w[:, cols]


def kernel(x_prompt, x_sample, cache_k, cache_v, state_ret, meta_tokens, rel_bias, norm1_g, w_in,
           lambda_q1, lambda_k1, lambda_q2, lambda_k2, diff_subln_g, w_ret_branch, w_diff_branch,
           w_o, norm2_g, w_ffn_up, w_ffn_down, normf_g, _phases=(1, 2, 3), _ret_raw=False):
    f = lambda a: np.ascontiguousarray(np.asarray(a, dtype=np.float32))
    x_prompt, x_sample, cache_k, cache_v, state_ret = map(f, (x_prompt, x_sample, cache_k, cache_v, state_ret))
    B, SEQ, _ = x_prompt.shape
    DB = x_sample.shape[0]
    P = cache_k.shape[2]
    NT = SEQ // TS
    NOWN = NT // 2
    ncores = 2 * B
    assert DB == 2 * ncores
    LTOK = NMETA + SEQ
    w_in0 = f(w_in)[0]
    w_inx = np.ascontiguousarray(np.concatenate(
        [w_in0, _swap_halves_cols(w_in0, C_RQ, RH, RDK), _swap_halves_cols(w_in0, C_RK, RH, RDK)], axis=1))
    hc = _host_consts(SEQ, P)
    common = dict(
        meta=f(meta_tokens), w_in=w_inx, w_rb=f(w_ret_branch)[0], w_db=f(w_diff_branch)[0], w_o=f(w_o)[0],
        w_up=f(w_ffn_up)[0], w_dn=f(w_ffn_down)[0],
        g1row=f(norm1_g)[0].reshape(1, D), g2row=f(norm2_g)[0].reshape(1, D),
        subg=f(diff_subln_g)[0].reshape(128, 1), normf=f(normf_g).reshape(1, D),
        lamv=np.concatenate([f(lambda_q1)[0], f(lambda_k1)[0], f(lambda_q2)[0], f(lambda_k2)[0]]).reshape(1, 4 * DDH),
        relb=f(rel_bias), **hc)
    nc, cx = build(SEQ, P, phases=_phases)
    cc = [_core_consts(g, NOWN) for g in range(2)]
    in_maps = []
    for c in range(ncores):
        b, g = c // 2, c % 2
        xa = x_prompt[b]
        xo = np.ascontiguousarray(xa.reshape(NT, TS, D)[g::2])
        m = dict(common)
        m.update(cc[g])
        m.update(x_all=xa, x_own=xo, xs_in=x_sample[2 * c:2 * c + 2],
                 ck_in=np.ascontiguousarray(cache_k[0, 2 * c:2 * c + 2].reshape(2, P, D)),
                 cv_in=np.ascontiguousarray(cache_v[0, 2 * c:2 * c + 2].reshape(2, P, D)),
                 st_in=np.ascontiguousarray(state_ret[0, 2 * c:2 * c + 2]))
        in_maps.append(m)
    res = run_bass_kernel_spmd(nc, in_maps, core_ids=list(range(ncores)))
    R = res.results
    if _ret_raw:
        return R
    k_rows_p = np.stack([R[2 * b]["k_rows"].reshape(LTOK, DH_, 2 * DDH) for b in range(B)])[None]
    v_rows_p = np.stack([R[2 * b]["v_rows"].reshape(LTOK, DH_, DDV) for b in range(B)])[None]
    ret_p = np.stack([R[2 * b]["ret_state"] for b in range(B)])[None]
    y_prompt = np.zeros((B, NT, TS, D), np.float32)
    for c in range(ncores):
        y_prompt[c // 2, (c % 2)::2] = R[c]["y_own"]
    y_prompt = y_prompt.reshape(B, SEQ, D)
    y_sample = np.concatenate([R[c]["y_s"].reshape(2, 16, D) for c in range(ncores)], axis=0)
    ks = np.concatenate([R[c]["k_rows_s"].reshape(2, 16, DH_, 2 * DDH) for c in range(ncores)], axis=0)[None]
    vs = np.concatenate([R[c]["v_rows_s"].reshape(2, 16, DH_, DDV) for c in range(ncores)], axis=0)[None]
    rs = np.concatenate([R[c]["ret_state_s"] for c in range(ncores)], axis=0)[None]
    return (y_prompt, y_sample, k_rows_p, v_rows_p, ret_p, ks, vs, rs)
```
